# Optimizing a Trainium2 kernel written in Bass

```python
import jax, jax.numpy as jnp
from jax import lax
import numpy as np

D_MODEL = 1024
BATCH = 8
SEQ = 8192
DEPTH = 4

CTX_LEN = 256
GRID_W = 64
D_MIX = D_MODEL
MLA_HEADS = 8
MLA_NOPE = 64
MLA_ROPE = 32
MLA_V = 64
MLA_Q_RANK = 384
MLA_KV_RANK = 256
MLA_OUT = MLA_HEADS * MLA_V
ATTN_SCALE = (MLA_NOPE + MLA_ROPE) ** -0.5
Q_BLOCK = 128
ROPE_BASE = 10000.0
CONV_CH = 256
CONV_WIDTH = 31
SGU_HEADS = 4
SGU_HEAD_DIM = 64
SGU_CH = SGU_HEADS * SGU_HEAD_DIM
CHUNK = 128
D_FF = 4 * D_MODEL
EPS = 1e-6
IN_Q = MLA_Q_RANK
IN_KV = MLA_KV_RANK
IN_KR = MLA_ROPE
IN_CONV = 2 * CONV_CH
IN_SGU = 2 * SGU_CH
D_IN = IN_Q + IN_KV + IN_KR + IN_CONV + IN_SGU
IN_SPLITS = (IN_Q, IN_Q + IN_KV, IN_Q + IN_KV + IN_KR, IN_Q + IN_KV + IN_KR + IN_CONV)

kernel_name = "hybrid_mla_conv_sgu_dit_block"


def rmsnorm(x, g):
    xf = x.astype(jnp.float32)
    y = xf * lax.rsqrt(jnp.mean(xf * xf, -1, keepdims=True) + EPS)
    return (y * g.astype(jnp.float32)).astype(x.dtype)


def layernorm(x, g, b):
    xf = x.astype(jnp.float32)
    mu = jnp.mean(xf, -1, keepdims=True)
    var = jnp.mean(jnp.square(xf - mu), -1, keepdims=True)
    y = (xf - mu) * lax.rsqrt(var + EPS) * g.astype(jnp.float32) + b.astype(jnp.float32)
    return y.astype(x.dtype)


def modulate(h, shift, scale):
    return h * (1 + scale) + shift


def _rot(x, pos):
    n = x.shape[-1] // 2
    inv = 1.0 / (ROPE_BASE ** (jnp.arange(n, dtype=jnp.float32) / n))
    ang = pos.astype(jnp.float32)[:, None] * inv
    cos = jnp.cos(ang)[:, None, :]
    sin = jnp.sin(ang)[:, None, :]
    x1, x2 = x[..., :n], x[..., n:]
    return jnp.concatenate([x1 * cos - x2 * sin, x2 * cos + x1 * sin], -1).astype(x.dtype)


def rope_2d(x, row, col):
    h = x.shape[-1] // 2
    return jnp.concatenate([_rot(x[..., :h], row), _rot(x[..., h:], col)], -1)


def mla_q(z_q, qn_g, w_uq, row, col):
    b, l, _ = z_q.shape
    q = (rmsnorm(z_q, qn_g) @ w_uq).reshape(b, l, MLA_HEADS, MLA_NOPE + MLA_ROPE)
    q_nope, q_rope = q[..., :MLA_NOPE], q[..., MLA_NOPE:]
    if row is not None:
        q_rope = rope_2d(q_rope, row, col)
    return jnp.concatenate([q_nope, q_rope], -1)


def mla_kv(z_kv, z_kr, kvn_g, w_ukv, row, col):
    b, l, _ = z_kv.shape
    kv = (rmsnorm(z_kv, kvn_g) @ w_ukv).reshape(b, l, MLA_HEADS, MLA_NOPE + MLA_V)
    k_nope, v = kv[..., :MLA_NOPE], kv[..., MLA_NOPE:]
    k_rope = z_kr[:, :, None, :]
    if row is not None:
        k_rope = rope_2d(k_rope, row, col)
    k_rope = jnp.broadcast_to(k_rope, (b, l, MLA_HEADS, MLA_ROPE))
    return jnp.concatenate([k_nope, k_rope], -1), v


def attend(q, k, v):
    s = jnp.einsum('bqhd,bkhd->bhqk', q, k).astype(jnp.float32) * ATTN_SCALE
    p = jax.nn.softmax(s, axis=-1).astype(v.dtype)
    return jnp.einsum('bhqk,bkhd->bqhd', p, v)


def attend_blocked(q, k, v):
    b, l, h, dq = q.shape
    qb = q.reshape(b, l // Q_BLOCK, Q_BLOCK, h, dq).transpose(1, 0, 2, 3, 4)
    out = lax.map(lambda qi: attend(qi, k, v), qb)
    return out.transpose(1, 0, 2, 3, 4).reshape(b, l, h * v.shape[-1])


def conv_module(z, w, b, ln_g, ln_b):
    a, gate = jnp.split(z, 2, axis=-1)
    y = a * jax.nn.sigmoid(gate)
    y = lax.conv_general_dilated(
        y, w[:, None, :].astype(y.dtype), window_strides=(1,), padding='SAME',
        dimension_numbers=('NWC', 'WIO', 'NWC'), feature_group_count=CONV_CH) + b
    return jax.nn.silu(layernorm(y, ln_g, ln_b))


def sgu_module(z, ln_g, ln_b, w_s, b_s):
    z = jax.nn.gelu(z)
    u, v = jnp.split(z, 2, axis=-1)
    v = layernorm(v, ln_g, ln_b)
    bsz, l, _ = v.shape
    v = v.reshape(bsz, l // CHUNK, CHUNK, SGU_HEADS, SGU_HEAD_DIM)
    mixed = jnp.einsum('hpq,bnqhd->bnphd', w_s, v) + b_s.T[None, None, :, :, None]
    return u * mixed.reshape(bsz, l, SGU_CH)


def sq_relu_mlp(h, w1, w2):
    return jnp.square(jax.nn.relu(h @ w1)) @ w2


def setup_inputs(seed: int = 0) -> dict:
    key = jax.random.key(seed)
    ks = jax.random.split(key, 32)
    f32 = jnp.float32
    L = DEPTH

    def nrm(k, shape, scale):
        return jax.random.normal(k, shape, f32) * scale

    def gain(k, shape):
        return 1.0 + 0.05 * jax.random.normal(k, shape, f32)

    return {
        "x": nrm(ks[0], (BATCH, SEQ, D_MODEL), 1.0),
        "c": nrm(ks[1], (BATCH, D_MODEL), 1.0),
        "ctx": nrm(ks[2], (BATCH, CTX_LEN, D_MODEL), 1.0),
        "c_ctx": nrm(ks[3], (D_MODEL,), 1.0),
        "ada_w": nrm(ks[4], (L, D_MODEL, 6 * D_MODEL), 0.5 * D_MODEL ** -0.5),
        "ada_b": nrm(ks[5], (L, 6 * D_MODEL), 0.02),
        "norm1_g": gain(ks[6], (L, D_MODEL)),
        "norm2_g": gain(ks[7], (L, D_MODEL)),
        "w_in": nrm(ks[8], (L, D_MODEL, D_IN), D_MODEL ** -0.5),
        "q_norm_g": gain(ks[9], (L, MLA_Q_RANK)),
        "w_uq": nrm(ks[10], (L, MLA_Q_RANK, MLA_HEADS * (MLA_NOPE + MLA_ROPE)), MLA_Q_RANK ** -0.5),
        "kv_norm_g": gain(ks[11], (L, MLA_KV_RANK)),
        "w_ukv": nrm(ks[12], (L, MLA_KV_RANK, MLA_HEADS * (MLA_NOPE + MLA_V)), MLA_KV_RANK ** -0.5),
        "conv_w": nrm(ks[13], (L, CONV_WIDTH, CONV_CH), CONV_WIDTH ** -0.5),
        "conv_b": nrm(ks[14], (L, CONV_CH), 0.02),
        "conv_ln_g": gain(ks[15], (L, CONV_CH)),
        "conv_ln_b": nrm(ks[16], (L, CONV_CH), 0.02),
        "sgu_ln_g": gain(ks[17], (L, SGU_CH)),
        "sgu_ln_b": nrm(ks[18], (L, SGU_CH), 0.02),
        "sgu_w": nrm(ks[19], (L, SGU_HEADS, CHUNK, CHUNK), CHUNK ** -0.5),
        "sgu_b": gain(ks[20], (L, SGU_HEADS, CHUNK)),
        "w_out": nrm(ks[21], (L, D_MIX, D_MODEL), D_MIX ** -0.5),
        "w_ff1": nrm(ks[22], (L, D_MODEL, D_FF), D_MODEL ** -0.5),
        "w_ff2": nrm(ks[23], (L, D_FF, D_MODEL), D_FF ** -0.5),
        "final_g": gain(ks[24], (D_MODEL,)),
    }


def reference(x, c, ctx, c_ctx, ada_w, ada_b, norm1_g, norm2_g, w_in, q_norm_g, w_uq,
              kv_norm_g, w_ukv, conv_w, conv_b, conv_ln_g, conv_ln_b, sgu_ln_g, sgu_ln_b,
              sgu_w, sgu_b, w_out, w_ff1, w_ff2, final_g):
    bsz, seq, _ = x.shape
    n_ctx = ctx.shape[1]
    rows = seq // GRID_W
    row = jnp.repeat(jnp.arange(rows, dtype=jnp.int32), GRID_W)
    col = jnp.tile(jnp.arange(GRID_W, dtype=jnp.int32), rows)
    s_lat = jax.nn.silu(c)
    s_ctx = jax.nn.silu(c_ctx)
    cx = ctx
    for l in range(DEPTH):
        last = l == DEPTH - 1
        m_x = (s_lat @ ada_w[l] + ada_b[l])[:, None, :]
        m_c = s_ctx @ ada_w[l] + ada_b[l]
        sh1, sc1, g1, sh2, sc2, g2 = jnp.split(m_x, 6, axis=-1)
        csh1, csc1, cg1, csh2, csc2, cg2 = jnp.split(m_c, 6, axis=-1)

        zx = modulate(rmsnorm(x, norm1_g[l]), sh1, sc1) @ w_in[l]
        zc = modulate(rmsnorm(cx, norm1_g[l]), csh1, csc1) @ w_in[l]
        xq, xkv, xkr, xconv, xsgu = jnp.split(zx, IN_SPLITS, axis=-1)
        cq, ckv, ckr, cconv, csgu = jnp.split(zc, IN_SPLITS, axis=-1)

        q_x = mla_q(xq, q_norm_g[l], w_uq[l], row, col)
        k_x, v_x = mla_kv(xkv, xkr, kv_norm_g[l], w_ukv[l], row, col)
        k_c, v_c = mla_kv(ckv, ckr, kv_norm_g[l], w_ukv[l], None, None)
        a_x = attend_blocked(q_x, jnp.concatenate([k_c, k_x], 1), jnp.concatenate([v_c, v_x], 1))
        y_x = jnp.concatenate([
            a_x,
            conv_module(xconv, conv_w[l], conv_b[l], conv_ln_g[l], conv_ln_b[l]),
            sgu_module(xsgu, sgu_ln_g[l], sgu_ln_b[l], sgu_w[l], sgu_b[l]),
        ], axis=-1) @ w_out[l]
        x = x + g1 * y_x

        if not last:
            q_c = mla_q(cq, q_norm_g[l], w_uq[l], None, None)
            a_c = attend(q_c, k_c, v_c).reshape(bsz, n_ctx, MLA_OUT)
            y_c = jnp.concatenate([
                a_c,
                conv_module(cconv, conv_w[l], conv_b[l], conv_ln_g[l], conv_ln_b[l]),
                sgu_module(csgu, sgu_ln_g[l], sgu_ln_b[l], sgu_w[l], sgu_b[l]),
            ], axis=-1) @ w_out[l]
            cx = cx + cg1 * y_c

        x = x + g2 * sq_relu_mlp(modulate(rmsnorm(x, norm2_g[l]), sh2, sc2), w_ff1[l], w_ff2[l])
        if not last:
            cx = cx + cg2 * sq_relu_mlp(modulate(rmsnorm(cx, norm2_g[l]), csh2, csc2), w_ff1[l], w_ff2[l])

    return rmsnorm(x, final_g)
```

```python
import numpy as np
from contextlib import ExitStack
import concourse.bass as bass
import concourse.mybir as mybir
from concourse.bass_utils import run_bass_kernel_spmd

F32 = mybir.dt.float32
BF16 = mybir.dt.bfloat16
ALU = mybir.AluOpType
AF = mybir.ActivationFunctionType
AX = mybir.AxisListType

D = 1024
NCTX = 256
GRID_W = 64
HEADS = 8
DQK = 96
DV = 64
QR = 384
KVR = 256
D_IN = 1696
DFF = 4096
EPS = 1e-6
ATTN_SCALE = 96 ** -0.5
CW = 31
VL = 137
O_N1G, O_N2G, O_QNG, O_KVNG, O_CONVB, O_CLNG, O_CLNB, O_ADAB, O_CONVW = 0, 8, 16, 19, 21, 23, 25, 27, 75


class Buf:
    __slots__ = ("name", "w", "r", "accum", "dsem", "dcnt", "psum")

    def __init__(self, name, accum=False, psum=False):
        self.name = name
        self.w = {}
        self.r = {}
        self.accum = accum
        self.psum = psum
        self.dsem = None
        self.dcnt = 0


def _merge(d, s):
    for k, v in s.items():
        o = d.get(k)
        if o is None or o[1] < v[1]:
            d[k] = v


class Eng:
    def __init__(self, nc, e, name, compute=True):
        self.e = e
        self.name = name
        self.sem = nc.alloc_semaphore("s_" + name) if compute else None
        self.cnt = 0
        self.seen = {}

    def wait(self, deps):
        for k, (sem, val) in deps.items():
            if self.seen.get(k, 0) < val:
                self.e.wait_ge(sem, val)
                self.seen[k] = val


class Ctx:
    def __init__(self, nc):
        self.nc = nc
        self.PE = Eng(nc, nc.tensor, "pe")
        self.ACT = Eng(nc, nc.scalar, "act")
        self.DVE = Eng(nc, nc.vector, "dve")
        self.POOL = Eng(nc, nc.gpsimd, "pool")
        self.SP = Eng(nc, nc.sync, "sp", compute=False)
        self.dsems = {}

    def _deps(self, reads, writes, E=None):
        deps = {}
        for b in reads:
            _merge(deps, b.w)
            if b.psum:
                own = id(E.sem) if (E is not None and E.sem is not None) else None
                _merge(deps, {k: v for k, v in b.r.items() if k != own})
        for b in writes:
            _merge(deps, b.r)
            if not b.accum:
                _merge(deps, b.w)
        return deps

    def _record(self, tok, reads, writes):
        key = id(tok[0])
        for b in reads:
            o = b.r.get(key)
            if o is None or o[1] < tok[1]:
                b.r[key] = tok
        for b in writes:
            if b.accum:
                o = b.w.get(key)
                if o is None or o[1] < tok[1]:
                    b.w[key] = tok
            else:
                b.w = {key: tok}
                b.r = {}

    def op(self, E, fn, reads=(), writes=()):
        E.wait(self._deps(reads, writes, E))
        ins = fn(E.e)
        E.cnt += 1
        ins.then_inc(E.sem, 1)
        self._record((E.sem, E.cnt), reads, writes)

    def mm(self, ps_buf, mms, reads, last_reads=()):
        E = self.PE
        E.wait(self._deps(reads, [ps_buf]))
        n = len(mms)
        ins = None
        for i, (o, l, r) in enumerate(mms):
            ins = E.e.matmul(o, l, r, start=(i == 0), stop=(i == n - 1))
        E.cnt += 1
        ins.then_inc(E.sem, 1)
        self._record((E.sem, E.cnt), reads, [ps_buf])

    def mm_acc(self, ps_buf, o, l, r, reads, start, stop, mark=True):
        E = self.PE
        if not mark:
            deps = {}
            for b in reads:
                _merge(deps, b.w)
            if start:
                _merge(deps, ps_buf.r)
                _merge(deps, ps_buf.w)
            E.wait(deps)
            E.e.matmul(o, l, r, start=start, stop=stop)
            if start:
                ps_buf.r = {}
            return
        deps = {}
        for b in reads:
            _merge(deps, b.w)
        if start:
            _merge(deps, ps_buf.r)
            _merge(deps, ps_buf.w)
        E.wait(deps)
        ins = E.e.matmul(o, l, r, start=start, stop=stop)
        E.cnt += 1
        ins.then_inc(E.sem, 1)
        tok = (E.sem, E.cnt)
        key = id(tok[0])
        for b in reads:
            b.r[key] = tok
        if start:
            ps_buf.r = {}
        ps_buf.w = {key: tok}

    def dma(self, Q, out, in_, reads, writes, owner):
        Q.wait(self._deps(reads, writes))
        ent = self.dsems.get(owner.name)
        if ent is None:
            ent = [self.nc.alloc_semaphore("d_" + owner.name), 0]
            self.dsems[owner.name] = ent
        ins = Q.e.dma_start(out=out, in_=in_)
        ent[1] += 16
        ins.then_inc(ent[0], 16)
        self._record((ent[0], ent[1]), reads, writes)

    def barrier(self):
        deps = {}
        for E in (self.PE, self.ACT, self.DVE, self.POOL):
            if E.cnt:
                deps[id(E.sem)] = (E.sem, E.cnt)
        for ent in self.dsems.values():
            deps[id(ent[0])] = (ent[0], ent[1])
        for E in (self.PE, self.ACT, self.DVE, self.POOL, self.SP):
            E.wait(deps)


class _Stop(Exception):
    pass


def build_program(S, DEPTH, stop=None):
    try:
        return _build(S, DEPTH, stop)
    except _Stop as e:
        nc, cx, es_global = e.payload
        cx.barrier()
        return nc


def _build(S, DEPTH, stop=None):
    NT = NCTX + S
    NBLK = NT // 128
    GSW = NT + 64
    nc = bass.Bass("TRN2", target_bir_lowering=False)
    cx = Ctx(nc)
    PE, ACT, DVE, POOL, SP = cx.PE, cx.ACT, cx.DVE, cx.POOL, cx.SP
    L = DEPTH

    def din(name, shape, dt=F32):
        return nc.dram_tensor(name, list(shape), dt, kind="ExternalInput").ap()

    def dscr(name, shape, dt):
        return nc.dram_tensor(name, list(shape), dt).ap()

    xT_in = din("xT", [D, NT])
    vecs_in = din("vecs", [128, L * VL + 24])
    ropeC_in = din("ropeC", [32, NT])
    ropeS_in = din("ropeS", [32, NT])
    ident_in = din("ident", [128, 128])
    ada_w_in = din("ada_w", [L, D, 6 * D])
    w_in_in = din("w_in", [L, D, D_IN])
    w_uq_in = din("w_uq", [L, QR, HEADS * DQK])
    w_ukv_in = din("w_ukv", [L, KVR, HEADS * 128])
    wsT_in = din("wsT", [L, 128, 4, 128])
    sgb_in = din("sgb", [L, 128, 2, 256])
    bsT_in = din("bsT", [L, 128, 2, 128])
    w_out_in = din("w_out", [L, D, D])
    w_ff1_in = din("w_ff1", [L, D, DFF])
    w_ff2_in = din("w_ff2", [L, DFF, D])
    outT = nc.dram_tensor("outT", [D, S], F32, kind="ExternalOutput").ap()

    XS = dscr("XS", [D, NT], F32)
    QS = dscr("QS", [HEADS, 97, NT], BF16)
    KS = dscr("KS", [HEADS, 64, NT], BF16)
    KR = dscr("KR", [32, NT], BF16)
    KONE = dscr("KONE", [1, NT], BF16)
    VS = dscr("VS", [HEADS, 128, NBLK, 65], BF16)
    GS = dscr("GS", [256, GSW], BF16)
    YS = dscr("YS", [D, NT], BF16)
    QN2 = dscr("QN2", [8, NT], F32)
    WinS = dscr("WinS", [L, D, D_IN], BF16)
    WuqS = dscr("WuqS", [L, QR, HEADS * DQK], BF16)
    WukvS = dscr("WukvS", [L, KVR, HEADS * 128], BF16)
    WoutS = dscr("WoutS", [L, D, D], BF16)
    W1S = dscr("W1S", [L, 8, 128, 8, 512], BF16)
    W2S = dscr("W2S", [L, 8, 128, 32, 128], BF16)
    WsS = dscr("WsS", [L, 128, 4, 128], BF16)
    bXS, bQS, bKS, bKR, bKONE, bVS, bGS, bYS, bQN2 = [Buf(n, accum=True) for n in
                                                    ("XS", "QS", "KS", "KR", "KONE", "VS", "GS", "YS", "QN2")]
    bWs = [Buf(f"Ws{l}", accum=True) for l in range(L)]
    bWb = [Buf(f"Wb{l}", accum=True) for l in range(L)]

    es_global = ExitStack()

    uid = [0]

    def sb(es, name, shape, dt):
        uid[0] += 1
        return es.enter_context(nc.sbuf_tensor(f"t{uid[0]}_{name}", list(shape), dt))

    PSALL = es_global.enter_context(nc.psum_tensor("psall", [128, 4096], F32))
    bPS = [Buf(f"bank{i}", psum=True) for i in range(8)]

    def bank(i):
        return PSALL[:, i * 512:(i + 1) * 512]

    vecs = sb(es_global, "vecs", [128, L * VL + 24], F32)
    bvecs = Buf("vecs")
    ident = sb(es_global, "ident", [128, 128], F32)
    bident = Buf("ident")
    onesD = sb(es_global, "onesD", [128, 128], F32)
    ones384 = sb(es_global, "ones384", [128, 128], F32)
    onesDb = sb(es_global, "onesDb", [128, 128], BF16)
    ones256 = sb(es_global, "ones256", [128, 128], F32)
    ones1 = sb(es_global, "ones1", [128, 64], F32)
    ind = sb(es_global, "ind", [128, 8, 8], BF16)
    ones8 = sb(es_global, "ones8", [128, 8], BF16)
    epsc = sb(es_global, "epsc", [128, 1], F32)
    onec = sb(es_global, "onec", [128, 1], F32)
    onesb = sb(es_global, "onesb", [128, 1024], BF16)
    zerob = sb(es_global, "zerob", [128, 64], BF16)
    bconst = Buf("const")
    s_sb = sb(es_global, "s_sb", [128, 8, 2], F32)
    bs_sb = Buf("s_sb")
    mod = sb(es_global, "mod", [128, 48, 2], F32)
    bmod = Buf("mod")
    gm = sb(es_global, "gm", [128, 2, 8, 2], F32)
    bgm = Buf("gm")
    kmax2 = sb(es_global, "kmax2", [8, 1], F32)
    bkmax = Buf("kmax2")
    stage = [sb(es_global, f"stage{i}", [128, 4096], BF16) for i in range(2)]
    bstage = [Buf(f"stage{i}") for i in range(2)]

    tiles = [(0, NCTX, True)] + [(NCTX + 512 * i, 512, False) for i in range(S // 512)]

    cx.dma(SP, vecs[:, :], vecs_in, [], [bvecs], bvecs)
    cx.dma(SP, ident[:, :], ident_in, [], [bident], bident)
    cx.op(DVE, lambda e: e.memset(onesD[:, :], 1.0 / D), [], [bconst])
    cx.op(DVE, lambda e: e.memset(ones384[:, :], 1.0 / QR), [], [bconst])
    cx.op(DVE, lambda e: e.memset(onesDb[:, :], 1.0 / D), [], [bconst])
    cx.op(DVE, lambda e: e.memset(ones256[:, :], 1.0 / 256), [], [bconst])
    cx.op(DVE, lambda e: e.memset(ones1[:, :], 1.0), [], [bconst])
    cx.op(DVE, lambda e: e.memset(ind[:, :, :], 0.0), [], [bconst])
    for h in range(8):
        cx.op(DVE, lambda e, h=h: e.memset(ind[:, h, h:h + 1], 1.0), [], [bconst])
    cx.op(DVE, lambda e: e.memset(ones8[:, :], 1.0), [], [bconst])
    cx.op(DVE, lambda e: e.memset(epsc[:, :], EPS), [], [bconst])
    cx.op(DVE, lambda e: e.memset(onec[:, :], 1.0), [], [bconst])
    cx.op(DVE, lambda e: e.memset(onesb[:, :], 1.0), [], [bconst])
    cx.op(DVE, lambda e: e.memset(zerob[:, :], 0.0), [], [bconst])
    cx.op(DVE, lambda e: e.memset(kmax2[:, :], 0.0), [], [bkmax])
    for c0 in range(0, NT, 1024):
        w = min(1024, NT - c0)
        cx.dma(SP, KONE[0:1, c0:c0 + w], onesb[0:1, 0:w], [bconst], [bKONE], bconst)
    for c in range(2):
        for (c0, w) in ((0, 16), (16 + NCTX, 32), (48 + NT, 16)):
            cx.dma(SP, GS[c * 128:(c + 1) * 128, c0:c0 + w], zerob[:, 0:w], [bconst], [bGS], bconst)
    SO = L * VL + 8
    cview = vecs[:, SO:SO + 16]
    s2 = s_sb[:, :, :].rearrange("p k j -> p (k j)")
    cx.op(ACT, lambda e: e.activation(out=s2, in_=cview, func=AF.Exp, scale=-1.0), [bvecs], [bs_sb])
    cx.op(DVE, lambda e: e.tensor_scalar_add(out=s2, in0=s2, scalar1=1.0), [bs_sb], [bs_sb])
    cx.op(DVE, lambda e: e.reciprocal(out=s2, in_=s2), [bs_sb], [bs_sb])
    cx.op(DVE, lambda e: e.tensor_tensor(out=s2, in0=s2, in1=cview, op=ALU.mult), [bs_sb, bvecs], [bs_sb])

    pieces = []
    for l in range(L):
        for k0 in range(0, 8, 2):
            pieces.append((w_in_in[l, k0 * 128:(k0 + 2) * 128, :].rearrange("(k p) n -> p k n", p=128),
                           WinS[l, k0 * 128:(k0 + 2) * 128, :].rearrange("(k p) n -> p k n", p=128), (2, D_IN), bWs[l]))
        pieces.append((w_uq_in[l].rearrange("(k p) n -> p k n", p=128),
                       WuqS[l].rearrange("(k p) n -> p k n", p=128), (3, 768), bWs[l]))
        pieces.append((w_ukv_in[l].rearrange("(k p) n -> p k n", p=128),
                       WukvS[l].rearrange("(k p) n -> p k n", p=128), (2, 1024), bWs[l]))
        pieces.append((wsT_in[l], WsS[l], (4, 128), bWs[l]))
    for l in range(L):
        for k0 in range(0, 8, 4):
            pieces.append((w_out_in[l, k0 * 128:(k0 + 4) * 128, :].rearrange("(k p) n -> p k n", p=128),
                           WoutS[l, k0 * 128:(k0 + 4) * 128, :].rearrange("(k p) n -> p k n", p=128), (4, 1024), bWb[l]))
        for g in range(8):
            pieces.append((w_ff1_in[l, :, g * 512:(g + 1) * 512].rearrange("(k p) n -> p k n", p=128),
                           W1S[l, g], (8, 512), bWb[l]))
        for k0 in range(0, 32, 4):
            pieces.append((w_ff2_in[l, k0 * 128:(k0 + 4) * 128, :].rearrange("(k p) (c n) -> p k c n", p=128, n=128),
                           W2S[l, :, :, k0:k0 + 4, :].rearrange("c p k n -> p k c n"), (4, 8, 128), bWb[l]))

    def stage_view(i, shp):
        n = int(np.prod(shp))
        v = stage[i % 2][:, 0:n]
        if len(shp) == 2:
            return v.rearrange("p (a b) -> p a b", a=shp[0])
        return v.rearrange("p (a b c) -> p a b c", a=shp[0], b=shp[1])

    def conv_load(i):
        src, dst, shp, bw_ = pieces[i]
        cx.dma(POOL, stage_view(i, shp), src, [], [bstage[i % 2]], bstage[i % 2])

    def conv_store(i):
        src, dst, shp, bw_ = pieces[i]
        if len(shp) == 3:
            sv = stage_view(i, shp)
            for c in range(shp[1]):
                cx.dma(POOL, dst[:, :, c, :], sv[:, :, c, :], [bstage[i % 2]], [bw_], bstage[i % 2])
        else:
            cx.dma(POOL, dst, stage_view(i, shp), [bstage[i % 2]], [bw_], bstage[i % 2])

    conv_load(0)
    for i in range(len(pieces)):
        if i + 1 < len(pieces):
            conv_load(i + 1)
        conv_store(i)

    def chk(name):
        if stop == name:
            e = _Stop()
            e.payload = (nc, cx, es_global)
            raise e

    chk("pro")
    for l in range(L):
        last = (l == L - 1)
        VB = l * VL
        Xsrc = xT_in if l == 0 else XS
        bXsrc = Buf("xin") if l == 0 else bXS

        with ExitStack() as es:
            adw = [sb(es, f"adw{i}", [128, 8, 512], F32) for i in range(2)]
            badw = [Buf(f"adw{i}") for i in range(2)]
            for g in range(12):
                cx.dma(SP, adw[g % 2][:, :, :],
                       ada_w_in[l, :, g * 512:(g + 1) * 512].rearrange("(k p) n -> p k n", p=128),
                       [], [badw[g % 2]], badw[g % 2])
                for jj in range(4):
                    j = g * 4 + jj
                    cx.mm(bPS[j % 2],
                          [(bank(j % 2)[:, 0:2], adw[g % 2][:, k, jj * 128:(jj + 1) * 128], s_sb[:, k, :]) for k in range(8)],
                          [badw[g % 2], bs_sb])
                    cx.op(DVE, lambda e, j=j: e.tensor_scalar(out=mod[:, j, :], in0=bank(j % 2)[:, 0:2],
                                                              scalar1=vecs[:, VB + O_ADAB + j:VB + O_ADAB + j + 1],
                                                              scalar2=None, op0=ALU.add),
                          [bPS[j % 2], bvecs], [bmod])
            for n, (og, osc) in enumerate(((O_N1G, 8), (O_N2G, 32))):
                for j in range(2):
                    cx.op(DVE, lambda e, n=n, j=j, osc=osc: e.tensor_scalar_add(out=gm[:, n, :, j], in0=mod[:, osc:osc + 8, j], scalar1=1.0),
                          [bmod], [bgm])
                    cx.op(DVE, lambda e, n=n, j=j, og=og: e.tensor_tensor(out=gm[:, n, :, j], in0=gm[:, n, :, j],
                                                                          in1=vecs[:, VB + og:VB + og + 8], op=ALU.mult),
                          [bgm, bvecs], [bgm])
            cx.barrier()
        chk(f"P0_{l}")

        def rstd_from(E_ps_ap, ps_buf, out_ap, out_buf, tmp_ap, tmp_buf):
            cx.op(ACT, lambda e: e.activation(out=tmp_ap, in_=E_ps_ap, func=AF.Ln, bias=epsc[:, :], scale=1.0),
                  [ps_buf, bconst], [tmp_buf])
            cx.op(ACT, lambda e: e.activation(out=out_ap, in_=tmp_ap, func=AF.Exp, scale=-0.5), [tmp_buf], [out_buf])

        def adanorm(es_tag, xt_t, bxt, W, n, j, h_t, bh, sqt, bsq, rstd, brstd, t32, bt32, statbank, sq_act=True, ones_mat=None):
            if ones_mat is None:
                ones_mat = onesD
            osh = 0 if n == 0 else 24
            for c in range(8):
                if sq_act:
                    cx.op(ACT, lambda e, c=c: e.activation(out=sqt[c % 2][:, :W], in_=xt_t[:, c, :W], func=AF.Square),
                          [bxt], [bsq[c % 2]])
                else:
                    cx.op(DVE, lambda e, c=c: e.tensor_tensor(out=sqt[c % 2][:, :W], in0=xt_t[:, c, :W], in1=xt_t[:, c, :W], op=ALU.mult),
                          [bxt], [bsq[c % 2]])
                cx.mm_acc(bPS[statbank], bank(statbank)[:, :W], ones_mat[:, :], sqt[c % 2][:, :W], [bsq[c % 2], bconst],
                          start=(c == 0), stop=(c == 7))
            rstd_from(bank(statbank)[:, :W], bPS[statbank], rstd[:, :W], brstd, t32[0][:, :W], bt32[0])
            for c in range(8):
                cx.op(DVE, lambda e, c=c: e.scalar_tensor_tensor(out=t32[c % 2][:, :W], in0=xt_t[:, c, :W],
                                                                  scalar=gm[:, n, c, j:j + 1], in1=rstd[:, :W],
                                                                  op0=ALU.mult, op1=ALU.mult),
                      [bxt, bgm, brstd], [bt32[c % 2]])
                cx.op(ACT, lambda e, c=c: e.activation(out=h_t[:, c, :W], in_=t32[c % 2][:, :W], func=AF.Identity,
                                                        bias=mod[:, osh + c, j:j + 1], scale=1.0),
                      [bt32[c % 2], bmod], [bh])

        with ExitStack() as es:
            w_in_sb = sb(es, "w_in_sb", [128, 8, D_IN + 32], BF16)
            w_uq_sb = sb(es, "w_uq_sb", [128, 3, 8, 128], BF16)
            w_ukv_sb = sb(es, "w_ukv_sb", [128, 2, 1024], BF16)
            wsT_sb = sb(es, "wsT_sb", [128, 4, 128], BF16)
            sgb_sb = sb(es, "sgb_sb", [128, 2, 256], F32)
            bsT_sb = sb(es, "bsT_sb", [128, 2, 128], F32)
            bw1 = Buf("w1")
            xt = [sb(es, f"xt{i}", [128, 8, 512], F32) for i in range(2)]
            bxt = [Buf(f"xt{i}") for i in range(2)]
            ropet = [sb(es, f"ropet{i}", [128, 2, 512], F32) for i in range(2)]
            bropet = [Buf(f"ropet{i}") for i in range(2)]
            sqt = [sb(es, f"sqt{i}", [128, 512], F32) for i in range(2)]
            bsq = [Buf(f"sq{i}") for i in range(2)]
            t32 = [sb(es, f"t32_{i}", [128, 512], F32) for i in range(4)]
            bt32 = [Buf(f"t32_{i}") for i in range(4)]
            rstd = sb(es, "rstd", [128, 512], F32)
            brstd = Buf("rstd")
            h_t2 = [sb(es, f"h_t{i}", [128, 8, 512], BF16) for i in range(2)]
            bh2 = [Buf(f"h{i}") for i in range(2)]
            sqtA = [sb(es, f"sqtA{i}", [128, 512], BF16) for i in range(2)]
            bsqA = [Buf(f"sqA{i}") for i in range(2)]
            t32A = [sb(es, f"t32A{i}", [128, 512], F32) for i in range(2)]
            bt32A = [Buf(f"t32A{i}") for i in range(2)]
            rstdA = sb(es, "rstdA", [128, 512], F32)
            brstdA = Buf("rstdA")
            zqn = sb(es, "zqn", [128, 3, 512], BF16)
            bzqn = Buf("zqn")
            kvn = sb(es, "kvn", [128, 2, 512], BF16)
            bkvn = Buf("kvn")
            qt = [sb(es, f"qt{i}", [128, 512], BF16) for i in range(2)]
            bqt = [Buf(f"qt{i}") for i in range(2)]
            sqq = [sb(es, f"sqq{i}", [128, 512], BF16) for i in range(2)]
            bsqq = [Buf(f"sqq{i}") for i in range(2)]
            kt = [sb(es, f"kt{i}", [128, 512], BF16) for i in range(2)]
            bkt = [Buf(f"kt{i}") for i in range(2)]
            krt = sb(es, "krt", [128, 512], BF16)
            bkrt = Buf("krt")
            vt = sb(es, "vt", [128, 8, 4, 65], BF16)
            bvt = Buf("vt")
            glt = sb(es, "glt", [128, 2, 512], BF16)
            bglt = Buf("glt")
            ut = sb(es, "ut", [128, 2, 512], BF16)
            but = Buf("ut")
            vnt = [sb(es, f"vnt{i}", [128, 256], BF16) for i in range(2)]
            bvnt = [Buf(f"vnt{i}") for i in range(2)]
            sgo = sb(es, "sgo", [128, 2, 512], BF16)
            bsgo = Buf("sgo")
            qn_sb = sb(es, "qn_sb", [8, 512], F32)
            bqn = Buf("qn_sb")
            kn_sb = sb(es, "kn_sb", [8, 1], F32)
            bkn = Buf("kn_sb")
            tv = sb(es, "tv", [128, 4, 3, 256], F32)
            btA = [Buf(f"tA{i}") for i in range(4)]
            btB = [Buf(f"tB{i}") for i in range(4)]
            btV = [Buf(f"tV{i}") for i in range(4)]
            stv = sb(es, "stv", [128, 4, 8], F32)
            bstv = [Buf(f"stv{i}") for i in range(4)]
            vnt4 = sb(es, "vnt4", [128, 4, 256], BF16)
            bvnt4 = [Buf(f"vnt4_{i}") for i in range(4)]

            cx.dma(SP, w_in_sb[:, :, 0:D_IN], WinS[l].rearrange("(k p) n -> p k n", p=128), [bWs[l]], [bw1], bw1)
            for k in range(3):
                cx.dma(SP, w_uq_sb[:, k, :, 0:96], WuqS[l, k * 128:(k + 1) * 128, :].rearrange("p (h d) -> p h d", d=96), [bWs[l]], [bw1], bw1)
            cx.dma(SP, w_ukv_sb[:, :, :], WukvS[l].rearrange("(k p) n -> p k n", p=128), [bWs[l]], [bw1], bw1)
            cx.dma(SP, wsT_sb[:, :, :], WsS[l], [bWs[l]], [bw1], bw1)
            cx.dma(SP, sgb_sb[:, :, :], sgb_in[l], [], [bw1], bw1)
            cx.dma(SP, bsT_sb[:, :, :], bsT_in[l], [], [bw1], bw1)
            cx.op(DVE, lambda e: e.memset(vt[:, :, :, 64:65], 1.0), [], [bvt])
            cx.op(DVE, lambda e: e.memset(kmax2[:, :], 0.0), [bkmax], [bkmax])
            for k in range(8):
                src = w_in_sb[:, k, 640:672].rearrange("p (a b j) -> p a b j", a=2, b=2)
                dst = w_in_sb[:, k, D_IN:D_IN + 32].rearrange("p (a b j) -> p a b j", a=2, b=2)
                for b_ in range(2):
                    cx.op(DVE, lambda e, src=src, dst=dst, b_=b_: e.tensor_copy(out=dst[:, :, b_, :], in_=src[:, :, 1 - b_, :]), [bw1], [bw1])
            for k in range(3):
                for b_ in range(2):
                    src = w_uq_sb[:, k, :, 64:96].rearrange("p h (a b j) -> p h a b j", a=2, b=2)
                    dst = w_uq_sb[:, k, :, 96:128].rearrange("p h (a b j) -> p h a b j", a=2, b=2)
                    for a_ in range(2):
                        cx.op(DVE, lambda e, src=src, dst=dst, b_=b_, a_=a_: e.tensor_copy(out=dst[:, :, a_, b_, :], in_=src[:, :, a_, 1 - b_, :]),
                              [bw1], [bw1])

            def load_x(ti):
                t0, W, isc = tiles[ti]
                s = ti % 2
                cx.dma(SP, xt[s][:, :, :W], Xsrc[:, t0:t0 + W].rearrange("(k p) t -> p k t", p=128), [bXsrc], [bxt[s]], bxt[s])

            def load_rope(ti):
                t0, W, isc = tiles[ti]
                s = ti % 2
                cx.dma(SP, ropet[s][64:96, 0, :W], ropeC_in[:, t0:t0 + W], [], [bropet[s]], bropet[s])
                cx.dma(SP, ropet[s][64:96, 1, :W], ropeS_in[:, t0:t0 + W], [], [bropet[s]], bropet[s])

            def gelu_from_psum(ps_ap, ps_buf, out_ap, out_buf, shape_sel):
                a, b_ = (t32[1], bt32[1]), (t32[2], bt32[2])
                A = shape_sel(a[0]); B = shape_sel(b_[0])
                cx.op(ACT, lambda e: e.activation(out=A, in_=ps_ap, func=AF.Square, scale=0.21145921), [ps_buf], [a[1]])
                cx.op(DVE, lambda e: e.scalar_tensor_tensor(out=A, in0=A, scalar=1.0, in1=ps_ap, op0=ALU.add, op1=ALU.mult), [a[1], ps_buf], [a[1]])
                cx.op(ACT, lambda e: e.activation(out=B, in_=A, func=AF.Exp, scale=-2.0 * 0.7978845608028654), [a[1]], [b_[1]])
                cx.op(ACT, lambda e: e.activation(out=B, in_=B, func=AF.Identity, bias=onec[:, :], scale=1.0), [b_[1], bconst], [b_[1]])
                cx.op(DVE, lambda e: e.reciprocal(out=B, in_=B), [b_[1]], [b_[1]])
                cx.op(DVE, lambda e: e.tensor_tensor(out=out_ap, in0=B, in1=ps_ap, op=ALU.mult), [b_[1], ps_buf], [out_buf])

            def stageA(ti):
                t0, W, isc = tiles[ti]
                s = ti % 2
                j = 1 if isc else 0
                adanorm(es, xt[s], bxt[s], W, 0, j, h_t2[s], bh2[s], sqtA, bsqA, rstdA, brstdA, t32A, bt32A, 0, ones_mat=onesDb)

            def tile_funcs(ti):
                t0, W, isc = tiles[ti]
                s = ti % 2
                j = 1 if isc else 0
                nb = W // 128
                h_t = h_t2[s]
                bh = bh2[s]

                def proj(bk, col0, M, po=0):
                    cx.mm(bPS[bk], [(bank(bk)[po:po + M, :W], w_in_sb[:, k, col0:col0 + M], h_t[:, k, :W]) for k in range(8)], [bw1, bh])

                def B1():
                    for c in range(3):
                        proj(2 + c, c * 128, 128)
                    for c in range(3):
                        cx.op(ACT, lambda e, c=c: e.activation(out=sqt[c % 2][:, :W], in_=bank(2 + c)[:, :W], func=AF.Square), [bPS[2 + c]], [bsq[c % 2]])
                        cx.mm_acc(bPS[0], bank(0)[:, :W], ones384[:, :], sqt[c % 2][:, :W], [bsq[c % 2], bconst], start=(c == 0), stop=(c == 2))
                    rstd_from(bank(0)[:, :W], bPS[0], rstd[:, :W], brstd, t32[0][:, :W], bt32[0])
                    for c in range(3):
                        cx.op(DVE, lambda e, c=c: e.scalar_tensor_tensor(out=zqn[:, c, :W], in0=bank(2 + c)[:, :W],
                                                                          scalar=vecs[:, VB + O_QNG + c:VB + O_QNG + c + 1], in1=rstd[:, :W],
                                                                          op0=ALU.mult, op1=ALU.mult),
                              [bPS[2 + c], bvecs, brstd], [bzqn])
                    chk(f"q1_{l}_{ti}")
                    for hh in range(HEADS):
                        ba, bb = 2 + 2 * (hh % 2), 3 + 2 * (hh % 2)
                        qs = hh % 2
                        cx.mm(bPS[ba], [(bank(ba)[0:96, :W], w_uq_sb[:, k, hh, 0:96], zqn[:, k, :W]) for k in range(3)], [bw1, bzqn])
                        cx.mm(bPS[bb], [(bank(bb)[64:96, :W], w_uq_sb[:, k, hh, 96:128], zqn[:, k, :W]) for k in range(3)], [bw1, bzqn])
                        cx.op(ACT, lambda e, ba=ba, qs=qs: e.activation(out=qt[qs][0:64, :W], in_=bank(ba)[0:64, :W], func=AF.Copy), [bPS[ba]], [bqt[qs]])
                        cx.op(ACT, lambda e, ba=ba, qs=qs: e.activation(out=sqq[qs][0:96, :W], in_=bank(ba)[0:96, :W], func=AF.Square), [bPS[ba]], [bsqq[qs]])
                        chk(f"q2_{l}_{ti}")
                        cx.op(DVE, lambda e, ba=ba: e.tensor_tensor(out=t32[0][64:96, :W], in0=bank(ba)[64:96, :W], in1=ropet[s][64:96, 0, :W], op=ALU.mult),
                              [bPS[ba], bropet[s]], [bt32[0]])
                        chk(f"q2a_{l}_{ti}")
                        cx.op(DVE, lambda e, bb=bb: e.tensor_tensor(out=t32[1][64:96, :W], in0=bank(bb)[64:96, :W], in1=ropet[s][64:96, 1, :W], op=ALU.mult),
                              [bPS[bb], bropet[s]], [bt32[1]])
                        chk(f"q2b_{l}_{ti}")
                        cx.op(DVE, lambda e, qs=qs: e.tensor_tensor(out=qt[qs][64:96, :W], in0=t32[0][64:96, :W], in1=t32[1][64:96, :W], op=ALU.add),
                              [bt32[0], bt32[1]], [bqt[qs]])
                        chk(f"q3_{l}_{ti}")
                        cx.mm_acc(bPS[1], bank(1)[0:8, :W], ind[0:96, hh, :], sqq[qs][0:96, :W], [bsqq[qs], bconst], start=(hh == 0), stop=(hh == 7))
                        chk(f"q4_{l}_{ti}")
                        cx.dma(SP, QS[hh, 0:96, t0:t0 + W], qt[qs][0:96, :W], [bqt[qs]], [bQS], bqt[qs])
                    cx.op(DVE, lambda e: e.tensor_copy(out=qn_sb[:, :W], in_=bank(1)[0:8, :W]), [bPS[1]], [bqn])
                    cx.dma(SP, QN2[:, t0:t0 + W], qn_sb[:, :W], [bqn], [bQN2], bqn)

                    chk(f"P1b_{l}_{ti}")
                    for c in range(2):
                        proj(6 + c, QR + c * 128, 128)
                    for c in range(2):
                        cx.op(ACT, lambda e, c=c: e.activation(out=sqt[c % 2][:, :W], in_=bank(6 + c)[:, :W], func=AF.Square), [bPS[6 + c]], [bsq[c % 2]])
                        cx.mm_acc(bPS[0], bank(0)[:, :W], ones256[:, :], sqt[c % 2][:, :W], [bsq[c % 2], bconst], start=(c == 0), stop=(c == 1))
                    rstd_from(bank(0)[:, :W], bPS[0], rstd[:, :W], brstd, t32[0][:, :W], bt32[0])
                    for c in range(2):
                        cx.op(DVE, lambda e, c=c: e.scalar_tensor_tensor(out=kvn[:, c, :W], in0=bank(6 + c)[:, :W],
                                                                          scalar=vecs[:, VB + O_KVNG + c:VB + O_KVNG + c + 1], in1=rstd[:, :W],
                                                                          op0=ALU.mult, op1=ALU.mult),
                              [bPS[6 + c], bvecs, brstd], [bkvn])
                    proj(6, 640, 32, po=64)
                    proj(7, D_IN, 32, po=64)
                    cx.op(DVE, lambda e: e.tensor_tensor(out=t32[0][64:96, :W], in0=bank(6)[64:96, :W], in1=ropet[s][64:96, 0, :W], op=ALU.mult),
                          [bPS[6], bropet[s]], [bt32[0]])
                    cx.op(DVE, lambda e: e.tensor_tensor(out=t32[1][64:96, :W], in0=bank(7)[64:96, :W], in1=ropet[s][64:96, 1, :W], op=ALU.mult),
                          [bPS[7], bropet[s]], [bt32[1]])
                    cx.op(DVE, lambda e: e.tensor_tensor(out=krt[64:96, :W], in0=t32[0][64:96, :W], in1=t32[1][64:96, :W], op=ALU.add),
                          [bt32[0], bt32[1]], [bkrt])
                    cx.dma(SP, KR[:, t0:t0 + W], krt[64:96, :W], [bkrt], [bKR], bkrt)
                    cx.op(ACT, lambda e: e.activation(out=sqq[0][64:96, :W], in_=krt[64:96, :W], func=AF.Square), [bkrt], [bsqq[0]])
                    cx.mm_acc(bPS[1], bank(1)[0:8, :W], ones8[64:96, :], sqq[0][64:96, :W], [bsqq[0], bconst], start=True, stop=False)
                    for hh in range(HEADS):
                        bk = 2 + (hh % 4)
                        ks = hh % 2
                        cx.mm(bPS[bk], [(bank(bk)[0:64, :W], w_ukv_sb[:, k, hh * 128:hh * 128 + 64], kvn[:, k, :W]) for k in range(2)], [bw1, bkvn])
                        cx.op(ACT, lambda e, bk=bk, ks=ks: e.activation(out=kt[ks][0:64, :W], in_=bank(bk)[0:64, :W], func=AF.Copy), [bPS[bk]], [bkt[ks]])
                        cx.op(ACT, lambda e, bk=bk, ks=ks: e.activation(out=sqq[1][0:64, :W], in_=bank(bk)[0:64, :W], func=AF.Square), [bPS[bk]], [bsqq[1]])
                        cx.mm_acc(bPS[1], bank(1)[0:8, :W], ind[0:64, hh, :], sqq[1][0:64, :W], [bsqq[1], bconst], start=False, stop=(hh == 7))
                        cx.dma(SP, KS[hh, :, t0:t0 + W], kt[ks][0:64, :W], [bkt[ks]], [bKS], bkt[ks])
                    cx.op(DVE, lambda e: e.tensor_reduce(out=kn_sb[:, :], in_=bank(1)[0:8, :W], axis=AX.X, op=ALU.max), [bPS[1]], [bkn])
                    cx.op(DVE, lambda e: e.tensor_tensor(out=kmax2[:, :], in0=kmax2[:, :], in1=kn_sb[:, :], op=ALU.max), [bkn, bkmax], [bkmax])
                    chk(f"P1c_{l}_{ti}")
                    nb = W // 128
                    for b_ in range(nb):
                        bk = 2 + b_
                        cx.mm(bPS[bk], [(bank(bk)[:, 0:512].rearrange("p (h d) -> p h d", h=8),
                                         kvn[:, k, b_ * 128:(b_ + 1) * 128],
                                         w_ukv_sb[:, k, :].rearrange("p (h c) -> p h c", c=128)[:, :, 64:128]) for k in range(2)], [bw1, bkvn])
                        cx.op(ACT, lambda e, bk=bk, b_=b_: e.activation(out=vt[:, :, b_, 0:64], in_=bank(bk)[:, 0:512].rearrange("p (h d) -> p h d", h=8), func=AF.Copy),
                              [bPS[bk]], [bvt])
                    blk0 = t0 // 128
                    for hh in range(HEADS):
                        cx.dma(SP, VS[hh, :, blk0:blk0 + nb, :], vt[:, hh, 0:nb, :], [bvt], [bVS], bvt)


                def CONV():
                    chk(f"P1d_{l}_{ti}")
                    for c in range(4):
                        proj(2 + c, 672 + c * 128, 128)
                    for c in range(2):
                        A = t32[2 + c][:, :W]
                        cx.op(ACT, lambda e, c=c, A=A: e.activation(out=A, in_=bank(4 + c)[:, :W], func=AF.Exp, scale=-1.0), [bPS[4 + c]], [bt32[2 + c]])
                        cx.op(ACT, lambda e, A=A: e.activation(out=A, in_=A, func=AF.Identity, bias=onec[:, :], scale=1.0), [bt32[2 + c], bconst], [bt32[2 + c]])
                        cx.op(DVE, lambda e, A=A: e.reciprocal(out=A, in_=A), [bt32[2 + c]], [bt32[2 + c]])
                        cx.op(DVE, lambda e, c=c, A=A: e.tensor_tensor(out=glt[:, c, :W], in0=A, in1=bank(2 + c)[:, :W], op=ALU.mult),
                              [bt32[2 + c], bPS[2 + c]], [bglt])
                    gcol = (16 if isc else 48) + t0
                    cx.dma(SP, GS[:, gcol:gcol + W].rearrange("(c p) t -> p c t", p=128), glt[:, :, :W], [bglt], [bGS], bglt)


                def SGU():
                    chk(f"P1e_{l}_{ti}")
                    for c in range(2):
                        proj(6 + c, 1184 + c * 128, 128)
                        gelu_from_psum(bank(6 + c)[:, :W], bPS[6 + c], ut[:, c, :W], but, lambda t: t[:, :W])
                    K2 = -2.0 * 0.7978845608028654
                    R = range(nb)
                    for b_ in R:
                        cx.mm(bPS[2 + b_], [(bank(2 + b_)[:, 0:256], h_t[:, k, b_ * 128:(b_ + 1) * 128], w_in_sb[:, k, 1440:1696]) for k in range(8)], [bw1, bh])
                    PSV = [bank(2 + b_)[:, 0:256] for b_ in R]
                    TA = [tv[:, b_, 0, :] for b_ in R]
                    TB = [tv[:, b_, 1, :] for b_ in R]
                    TV = [tv[:, b_, 2, :] for b_ in R]
                    for b_ in R:
                        cx.op(ACT, lambda e, b_=b_: e.activation(out=TA[b_], in_=PSV[b_], func=AF.Square, scale=0.21145921), [bPS[2 + b_]], [btA[b_]])
                    for b_ in R:
                        cx.op(DVE, lambda e, b_=b_: e.scalar_tensor_tensor(out=TA[b_], in0=TA[b_], scalar=1.0, in1=PSV[b_], op0=ALU.add, op1=ALU.mult),
                              [btA[b_], bPS[2 + b_]], [btA[b_]])
                    for b_ in R:
                        cx.op(ACT, lambda e, b_=b_: e.activation(out=TB[b_], in_=TA[b_], func=AF.Exp, scale=K2), [btA[b_]], [btB[b_]])
                    for b_ in R:
                        cx.op(ACT, lambda e, b_=b_: e.activation(out=TB[b_], in_=TB[b_], func=AF.Identity, bias=onec[:, :], scale=1.0), [btB[b_], bconst], [btB[b_]])
                    for b_ in R:
                        cx.op(DVE, lambda e, b_=b_: e.reciprocal(out=TB[b_], in_=TB[b_]), [btB[b_]], [btB[b_]])
                    for b_ in R:
                        cx.op(DVE, lambda e, b_=b_: e.scalar_tensor_tensor(out=TV[b_], in0=TB[b_], scalar=1.0, in1=PSV[b_], op0=ALU.mult, op1=ALU.mult,
                                                                            accum_out=stv[:, b_, 0:1]), [btB[b_], bPS[2 + b_]], [btV[b_], bstv[b_]])
                    for b_ in R:
                        cx.op(ACT, lambda e, b_=b_: e.activation(out=TA[b_], in_=TV[b_], func=AF.Square, accum_out=stv[:, b_, 2:3]), [btV[b_]], [btA[b_], bstv[b_]])
                    allst = [bstv[b_] for b_ in R]
                    cx.op(DVE, lambda e: e.tensor_scalar(out=stv[:, 0:nb, 1], in0=stv[:, 0:nb, 0], scalar1=1.0 / 256, scalar2=None, op0=ALU.mult), allst, allst)
                    cx.op(DVE, lambda e: e.tensor_tensor(out=stv[:, 0:nb, 5], in0=stv[:, 0:nb, 1], in1=stv[:, 0:nb, 1], op=ALU.mult), allst, allst)
                    cx.op(DVE, lambda e: e.scalar_tensor_tensor(out=stv[:, 0:nb, 3], in0=stv[:, 0:nb, 2], scalar=1.0 / 256, in1=stv[:, 0:nb, 5],
                                                                 op0=ALU.mult, op1=ALU.subtract), allst, allst)
                    cx.op(ACT, lambda e: e.activation(out=stv[:, 0:nb, 3], in_=stv[:, 0:nb, 3], func=AF.Ln, bias=epsc[:, :], scale=1.0), allst + [bconst], allst)
                    cx.op(ACT, lambda e: e.activation(out=stv[:, 0:nb, 4], in_=stv[:, 0:nb, 3], func=AF.Exp, scale=-0.5), allst, allst)
                    for b_ in R:
                        cx.op(DVE, lambda e, b_=b_: e.tensor_scalar(out=TV[b_], in0=TV[b_], scalar1=stv[:, b_, 1:2], scalar2=stv[:, b_, 4:5],
                                                                    op0=ALU.subtract, op1=ALU.mult), [btV[b_], bstv[b_]], [btV[b_]])
                    for b_ in R:
                        cx.op(DVE, lambda e, b_=b_: e.tensor_tensor(out=TV[b_], in0=TV[b_], in1=sgb_sb[:, 0, :], op=ALU.mult), [btV[b_], bw1], [btV[b_]])
                    for b_ in R:
                        cx.op(DVE, lambda e, b_=b_: e.tensor_tensor(out=vnt4[:, b_, :], in0=TV[b_], in1=sgb_sb[:, 1, :], op=ALU.add), [btV[b_], bw1], [bvnt4[b_]])

                    def tail():
                        for b_ in R:
                            for cc in range(2):
                                mb = 6 + cc
                                for hh2 in range(2):
                                    hd = 2 * cc + hh2
                                    if hh2 == 0:
                                        cx.mm(bPS[mb], [(bank(mb)[0:64, 0:128], vnt4[:, b_, hd * 64:(hd + 1) * 64], wsT_sb[:, hd, :])], [bvnt4[b_], bw1])
                                    else:
                                        PE.wait(cx._deps([bvnt4[b_], bw1], []))
                                        ins = PE.e.matmul(bank(mb)[64:128, 0:128], vnt4[:, b_, hd * 64:(hd + 1) * 64], wsT_sb[:, hd, :], start=True, stop=True)
                                        PE.cnt += 1
                                        ins.then_inc(PE.sem, 1)
                                        tok = (PE.sem, PE.cnt)
                                        bPS[mb].w = {id(PE.sem): tok}
                                        bvnt4[b_].r[id(PE.sem)] = tok
                                cx.op(DVE, lambda e, mb=mb, cc=cc, b_=b_: e.tensor_tensor(out=TB[b_][:, 0:128], in0=bank(mb)[:, 0:128], in1=bsT_sb[:, cc, :], op=ALU.add),
                                      [bPS[mb], bw1], [btB[b_]])
                                cx.op(DVE, lambda e, cc=cc, b_=b_: e.tensor_tensor(out=sgo[:, cc, b_ * 128:(b_ + 1) * 128], in0=TB[b_][:, 0:128],
                                                                                    in1=ut[:, cc, b_ * 128:(b_ + 1) * 128], op=ALU.mult),
                                      [btB[b_], but], [bsgo])
                        cx.dma(SP, YS[768:1024, t0:t0 + W].rearrange("(c p) t -> p c t", p=128), sgo[:, :, :W], [bsgo], [bYS], bsgo)
                    return tail

                return B1, CONV, SGU

            NTL = len(tiles)
            load_x(0)
            load_rope(0)
            if NTL > 1:
                load_x(1)
            stageA(0)
            tail_ = None
            for ti in range(NTL):
                if ti + 1 < NTL:
                    load_rope(ti + 1)
                fB1, fCONV, fSGU = tile_funcs(ti)
                fB1()
                if tail_ is not None:
                    tail_()
                    tail_ = None
                fCONV()
                if ti + 1 < NTL:
                    stageA(ti + 1)
                    if ti + 2 < NTL:
                        load_x(ti + 2)
                tail_ = fSGU()
            if tail_ is not None:
                tail_()
            cx.barrier()
        chk(f"P1_{l}")

        es_mid = ExitStack()
        es = es_mid
        if True:
            CH = 2048
            fq = [sb(es, f"fq{i}", [8, CH], F32) for i in range(2)]
            bfq = [Buf(f"fq{i}") for i in range(2)]
            fb = [sb(es, f"fb{i}", [8, CH], BF16) for i in range(2)]
            bfb = [Buf(f"fb{i}") for i in range(2)]
            for ci, c0 in enumerate(range(0, NT, CH)):
                w = min(CH, NT - c0)
                s = ci % 2
                cx.dma(SP, fq[s][:, :w], QN2[:, c0:c0 + w], [bQN2], [bfq[s]], bfq[s])
                cx.op(DVE, lambda e, s=s, w=w: e.tensor_scalar(out=fq[s][:, :w], in0=fq[s][:, :w], scalar1=kmax2[:, 0:1], scalar2=1e-30,
                                                               op0=ALU.mult, op1=ALU.add), [bfq[s], bkmax], [bfq[s]])
                cx.op(ACT, lambda e, s=s, w=w: e.activation(out=fq[s][:, :w], in_=fq[s][:, :w], func=AF.Ln), [bfq[s]], [bfq[s]])
                cx.op(ACT, lambda e, s=s, w=w: e.activation(out=fq[s][:, :w], in_=fq[s][:, :w], func=AF.Exp, scale=0.5), [bfq[s]], [bfq[s]])
                cx.op(DVE, lambda e, s=s, w=w: e.tensor_scalar(out=fb[s][:, :w], in0=fq[s][:, :w], scalar1=-1.0, scalar2=None, op0=ALU.mult),
                      [bfq[s]], [bfb[s]])
                cx.dma(SP, QS[:, 96, c0:c0 + w], fb[s][:, :w], [bfb[s]], [bQS], bfb[s])
        chk(f"FX_{l}")

        if True:
            diag = sb(es, "diag", [128, 2, CW, 128], BF16)
            bdiag = Buf("diag")
            gin = [sb(es, f"gin{i}", [128, 2, 544], BF16) for i in range(2)]
            bgin = [Buf(f"gin{i}") for i in range(2)]
            y32 = sb(es, "y32", [128, 2, 512], F32)
            by32 = Buf("y32")
            ysq = [sb(es, f"ysq{i}", [128, 512], F32) for i in range(2)]
            bysq = [Buf(f"ysq{i}") for i in range(2)]
            mean_sb = sb(es, "mean_sb", [128, 512], F32)
            bmean = Buf("mean")
            var_sb = sb(es, "var_sb", [128, 512], F32)
            bvar = Buf("var")
            ct = [sb(es, f"ct{i}", [128, 512], F32) for i in range(2)]
            bct = [Buf(f"ct{i}") for i in range(2)]
            co = [sb(es, f"co{i}", [128, 2, 512], BF16) for i in range(2)]
            bco = [Buf(f"co{i}") for i in range(2)]
            for c in range(2):
                for jt in range(CW):
                    cx.op(DVE, lambda e, c=c, jt=jt: e.tensor_scalar(out=diag[:, c, jt, :], in0=ident[:, :],
                                                                      scalar1=vecs[:, VB + O_CONVW + c * CW + jt:VB + O_CONVW + c * CW + jt + 1],
                                                                      scalar2=None, op0=ALU.mult), [bident, bvecs], [bdiag])
            p2tiles = [t for t in tiles if not (last and t[2])]

            def p2_load(i):
                t0, W, isc = p2tiles[i]
                gcol = (16 if isc else 48) + t0 - 15
                cx.dma(SP, gin[i % 2][:, :, 0:W + 30], GS[:, gcol:gcol + W + 30].rearrange("(c p) t -> p c t", p=128), [bGS], [bgin[i % 2]], bgin[i % 2])

            p2_load(0)
            for i, (t0, W, isc) in enumerate(p2tiles):
                s = i % 2
                if i + 1 < len(p2tiles):
                    p2_load(i + 1)
                for c in range(2):
                    cx.mm(bPS[c], [(bank(c)[:, :W], diag[:, c, jt, :], gin[s][:, c, jt:jt + W]) for jt in range(CW)], [bdiag, bgin[s]])
                    cx.op(ACT, lambda e, c=c: e.activation(out=y32[:, c, :W], in_=bank(c)[:, :W], func=AF.Identity,
                                                            bias=vecs[:, VB + O_CONVB + c:VB + O_CONVB + c + 1], scale=1.0), [bPS[c], bvecs], [by32])
                for c in range(2):
                    cx.mm_acc(bPS[2], bank(2)[:, :W], ones256[:, :], y32[:, c, :W], [by32, bconst], start=(c == 0), stop=(c == 1))
                cx.op(ACT, lambda e: e.activation(out=mean_sb[:, :W], in_=bank(2)[:, :W], func=AF.Copy), [bPS[2]], [bmean])
                for c in range(2):
                    cx.op(DVE, lambda e, c=c: e.tensor_tensor(out=y32[:, c, :W], in0=y32[:, c, :W], in1=mean_sb[:, :W], op=ALU.subtract), [by32, bmean], [by32])
                    cx.op(DVE, lambda e, c=c: e.tensor_tensor(out=ysq[c][:, :W], in0=y32[:, c, :W], in1=y32[:, c, :W], op=ALU.mult), [by32], [bysq[c]])
                    cx.mm_acc(bPS[3], bank(3)[:, :W], ones256[:, :], ysq[c][:, :W], [bysq[c], bconst], start=(c == 0), stop=(c == 1))
                rstd_from(bank(3)[:, :W], bPS[3], var_sb[:, :W], bvar, mean_sb[:, :W], bmean)
                for c in range(2):
                    T = ct[c][:, :W]
                    cx.op(DVE, lambda e, c=c: e.tensor_tensor(out=y32[:, c, :W], in0=y32[:, c, :W], in1=var_sb[:, :W], op=ALU.mult), [by32, bvar], [by32])
                    cx.op(DVE, lambda e, c=c: e.tensor_scalar(out=y32[:, c, :W], in0=y32[:, c, :W],
                                                              scalar1=vecs[:, VB + O_CLNG + c:VB + O_CLNG + c + 1],
                                                              scalar2=vecs[:, VB + O_CLNB + c:VB + O_CLNB + c + 1], op0=ALU.mult, op1=ALU.add), [by32, bvecs], [by32])
                    cx.op(ACT, lambda e, c=c, T=T: e.activation(out=T, in_=y32[:, c, :W], func=AF.Exp, scale=-1.0), [by32], [bct[c]])
                    cx.op(DVE, lambda e, T=T: e.tensor_scalar_add(out=T, in0=T, scalar1=1.0), [bct[c]], [bct[c]])
                    cx.op(DVE, lambda e, T=T: e.reciprocal(out=T, in_=T), [bct[c]], [bct[c]])
                    cx.op(DVE, lambda e, c=c, T=T: e.tensor_tensor(out=co[s][:, c, :W], in0=T, in1=y32[:, c, :W], op=ALU.mult), [bct[c], by32], [bco[s]])
                cx.dma(SP, YS[512:768, t0:t0 + W].rearrange("(c p) t -> p c t", p=128), co[s][:, :, :W], [bco[s]], [bYS], bco[s])
        chk(f"P2_{l}")

        if True:
            Kt = [sb(es, f"Kt{i}", [128, NT], BF16) for i in range(2)]
            bKt = [Buf(f"Kt{i}") for i in range(2)]
            Vt = [sb(es, f"Vt{i}", [128, NBLK, 65], BF16) for i in range(2)]
            bVt = [Buf(f"Vt{i}") for i in range(2)]
            qtt = [sb(es, f"qtt{i}", [128, 512], BF16) for i in range(3)]
            bqtt = [Buf(f"qtt{i}") for i in range(3)]
            GRP = 3
            pt = [sb(es, f"pt{i}", [128, 512 * GRP], BF16) for i in range(3)]
            bpt = [Buf(f"pt{i}") for i in range(3)]
            osb = [sb(es, f"osb{i}", [128, 512], F32) for i in range(2)]
            bosb = [Buf(f"osb{i}") for i in range(2)]
            rden = [sb(es, f"rden{i}", [128, 512], F32) for i in range(2)]
            brden = [Buf(f"rden{i}") for i in range(2)]
            ao = [sb(es, f"ao{i}", [128, 512], BF16) for i in range(2)]
            bao = [Buf(f"ao{i}") for i in range(2)]
            bpair = [Buf(f"pair{i}", psum=True) for i in range(2)]
            qtiles = [t for t in tiles if not (last and t[2])]
            blocks = [(hh, qi) for hh in range(HEADS) for qi in range(len(qtiles))]

            def load_kv(hh):
                s = hh % 2
                cx.dma(SP, Kt[s][0:64, :], KS[hh], [bKS], [bKt[s]], bKt[s])
                cx.dma(SP, Kt[s][64:96, :], KR, [bKR], [bKt[s]], bKt[s])
                cx.dma(SP, Kt[s][96:97, :], KONE, [bKONE], [bKt[s]], bKt[s])
                cx.dma(SP, Vt[s][:, :, :], VS[hh], [bVS], [bVt[s]], bVt[s])

            def load_q(bi):
                hh, qi = blocks[bi]
                t0, W, isc = qtiles[qi]
                cx.dma(SP, qtt[bi % 3][0:96, :W], QS[hh, 0:96, t0:t0 + W], [bQS], [bqtt[bi % 3]], bqtt[bi % 3])
                cx.dma(SP, qtt[bi % 3][96:97, :W], QS[hh, 96:97, t0:t0 + W], [bQS], [bqtt[bi % 3]], bqtt[bi % 3])

            steps = []
            for bi, (hh, qi) in enumerate(blocks):
                t0, W, isc = qtiles[qi]
                nblk = (NCTX // 128) if isc else NBLK
                grps = [list(range(g0, min(g0 + GRP, nblk))) for g0 in range(0, nblk, GRP)]
                for gi, blks in enumerate(grps):
                    steps.append((bi, hh, qi, gi, len(grps), blks))

            def slot_ap(n):
                return PSALL[:, (n % 2) * 512 * GRP:(n % 2 + 1) * 512 * GRP]

            def emit_S(n):
                bi, hh, qi, gi, ng, blks = steps[n]
                t0, W, isc = qtiles[qi]
                ps = slot_ap(n)
                E = PE
                wr_ = [bpair[n % 2]] + ([bPS[GRP * (n % 2) + i_] for i_ in range(GRP)] if n < 2 else [])
                E.wait(cx._deps([bKt[hh % 2], bqtt[bi % 3]], wr_))
                ins = None
                for jj, blk in enumerate(blks):
                    ins = E.e.matmul(ps[:, jj * W:(jj + 1) * W], Kt[hh % 2][0:97, blk * 128:(blk + 1) * 128], qtt[bi % 3][0:97, :W], start=True, stop=True)
                E.cnt += 1
                ins.then_inc(E.sem, 1)
                cx._record((E.sem, E.cnt), [bKt[hh % 2], bqtt[bi % 3]], [bpair[n % 2]])
                nw = len(blks) * W
                cx.op(ACT, lambda e: e.activation(out=pt[n % 3][:, 0:nw], in_=ps[:, 0:nw], func=AF.Exp, scale=ATTN_SCALE),
                      [bpair[n % 2]], [bpt[n % 3]])

            pend = []

            def emit_PV(n):
                bi, hh, qi, gi, ng, blks = steps[n]
                t0, W, isc = qtiles[qi]
                for jj, blk in enumerate(blks):
                    cx.mm_acc(bPS[6], bank(6)[0:65, :W], Vt[hh % 2][:, blk, :],
                              pt[n % 3][:, jj * W:(jj + 1) * W], [bVt[hh % 2], bpt[n % 3]],
                              start=(gi == 0 and jj == 0), stop=(gi == ng - 1 and jj == len(blks) - 1), mark=(jj == len(blks) - 1))
                if gi == ng - 1:
                    es_ = bi % 2
                    cx.op(DVE, lambda e: e.tensor_copy(out=osb[es_][0:65, :W], in_=bank(6)[0:65, :W]), [bPS[6]], [bosb[es_]])
                    cx.op(DVE, lambda e: e.reciprocal(out=rden[es_][64:65, :W], in_=osb[es_][64:65, :W]), [bosb[es_]], [brden[es_]])

                    def e2(es_=es_, W=W, hh=hh, t0=t0):
                        cx.mm(bPS[7], [(bank(7)[0:64, :W], ones1[64:65, 0:64], rden[es_][64:65, :W])], [bconst, brden[es_]])
                        cx.op(DVE, lambda e: e.tensor_tensor(out=ao[es_][0:64, :W], in0=osb[es_][0:64, :W], in1=bank(7)[0:64, :W], op=ALU.mult),
                              [bosb[es_], bPS[7]], [bao[es_]])
                        cx.dma(SP, YS[hh * 64:(hh + 1) * 64, t0:t0 + W], ao[es_][0:64, :W], [bao[es_]], [bYS], bao[es_])
                    pend.append([3, e2])

            load_kv(0)
            load_q(0)
            if len(blocks) > 1:
                load_q(1)
            LOOK = 1
            for n in range(len(steps) + LOOK):
                if n < len(steps):
                    bi, hh, qi, gi, ng, blks = steps[n]
                    if gi == 0:
                        if bi + 2 < len(blocks):
                            load_q(bi + 2)
                    emit_S(n)
                for p_ in pend:
                    p_[0] -= 1
                while pend and pend[0][0] <= 0:
                    pend.pop(0)[1]()
                if n - LOOK >= 0:
                    emit_PV(n - LOOK)
                    bi_, hh_, qi_, gi_, ng_, blks_ = steps[n - LOOK]
                    if qi_ == 0 and gi_ == 0 and hh_ + 1 < HEADS:
                        load_kv(hh_ + 1)
            while pend:
                pend.pop(0)[1]()
            cx.barrier()
        es_mid.close()
        chk(f"P3_{l}")

        with ExitStack() as es:
            w_out_sb = sb(es, "w_out_sb", [128, 8, D], BF16)
            bwo = Buf("w_out")
            w1b = [sb(es, f"w1b{i}", [128, 8, 512], BF16) for i in range(2)]
            bw1b = [Buf(f"w1b{i}") for i in range(2)]
            w2b = [sb(es, f"w2b{i}", [128, 32, 128], BF16) for i in range(2)]
            bw2b = [Buf(f"w2b{i}") for i in range(2)]
            xt = [sb(es, f"x4_{i}", [128, 8, 512], F32) for i in range(2)]
            bxt = [Buf(f"x4_{i}") for i in range(2)]
            yc = [sb(es, f"yc{i}", [128, 8, 512], BF16) for i in range(2)]
            byc = [Buf(f"yc{i}") for i in range(2)]
            h2 = sb(es, "h2", [128, 8, 512], BF16)
            bh2 = Buf("h2")
            u_t = sb(es, "u_t", [128, 32, 512], BF16)
            bu = [Buf(f"u{i}") for i in range(32)]
            rl = [sb(es, f"rl{i}", [128, 512], F32) for i in range(2)]
            brl = [Buf(f"rl{i}") for i in range(2)]
            sqt = [sb(es, f"sq4_{i}", [128, 512], BF16) for i in range(2)]
            bsq = [Buf(f"sq4_{i}") for i in range(2)]
            t32 = [sb(es, f"t4_{i}", [128, 512], F32) for i in range(2)]
            bt32 = [Buf(f"t4_{i}") for i in range(2)]
            rstd = sb(es, "rstd4", [128, 512], F32)
            brstd = Buf("rstd4")
            cx.dma(SP, w_out_sb[:, :, :], WoutS[l].rearrange("(k p) n -> p k n", p=128), [bWb[l]], [bwo], bwo)
            p4tiles = [t for t in tiles if not (last and t[2])]
            wcnt = [0, 0]

            def p4_load(i):
                t0, W, isc = p4tiles[i]
                s = i % 2
                cx.dma(SP, xt[s][:, :, :W], Xsrc[:, t0:t0 + W].rearrange("(k p) t -> p k t", p=128), [bXsrc], [bxt[s]], bxt[s])
                cx.dma(SP, yc[s][:, :, :W], YS[:, t0:t0 + W].rearrange("(k p) t -> p k t", p=128), [bYS], [byc[s]], byc[s])

            def load_w1(g, slot):
                cx.dma(SP, w1b[slot][:, :, :], W1S[l, g], [bWb[l]], [bw1b[slot]], bw1b[slot])

            def load_w2(c, slot):
                cx.dma(SP, w2b[slot][:, :, :], W2S[l, c], [bWb[l]], [bw2b[slot]], bw2b[slot])

            p4_load(0)
            load_w1(0, 0)
            for i, (t0, W, isc) in enumerate(p4tiles):
                s = i % 2
                j = 1 if isc else 0
                if i + 1 < len(p4tiles):
                    p4_load(i + 1)
                for c in range(8):
                    bk = c % 4
                    cx.mm(bPS[bk], [(bank(bk)[:, :W], w_out_sb[:, k, c * 128:(c + 1) * 128], yc[s][:, k, :W]) for k in range(8)], [bwo, byc[s]])
                    cx.op(DVE, lambda e, c=c, bk=bk: e.scalar_tensor_tensor(out=xt[s][:, c, :W], in0=bank(bk)[:, :W], scalar=mod[:, 16 + c, j:j + 1],
                                                                             in1=xt[s][:, c, :W], op0=ALU.mult, op1=ALU.add),
                          [bPS[bk], bmod, bxt[s]], [bxt[s]])
                adanorm(es, xt[s], bxt[s], W, 1, j, h2, bh2, sqt, bsq, rstd, brstd, t32, bt32, 4, sq_act=False, ones_mat=onesDb)
                for g in range(8):
                    slot = wcnt[0] % 2
                    wcnt[0] += 1
                    if g + 1 < 8:
                        load_w1(g + 1, wcnt[0] % 2)
                    else:
                        load_w2(0, wcnt[1] % 2)
                    for jj in range(4):
                        f = g * 4 + jj
                        bk = f % 4
                        cx.mm(bPS[bk], [(bank(bk)[:, :W], w1b[slot][:, k, jj * 128:(jj + 1) * 128], h2[:, k, :W]) for k in range(8)], [bw1b[slot], bh2])
                        cx.op(ACT, lambda e, bk=bk, f=f: e.activation(out=rl[f % 2][:, :W], in_=bank(bk)[:, :W], func=AF.Relu), [bPS[bk]], [brl[f % 2]])
                        cx.op(DVE, lambda e, f=f: e.tensor_tensor(out=u_t[:, f, :W], in0=rl[f % 2][:, :W], in1=rl[f % 2][:, :W], op=ALU.mult), [brl[f % 2]], [bu[f]])
                for c in range(8):
                    slot = wcnt[1] % 2
                    wcnt[1] += 1
                    if c + 1 < 8:
                        load_w2(c + 1, wcnt[1] % 2)
                    elif i + 1 < len(p4tiles):
                        load_w1(0, wcnt[0] % 2)
                    bk = 4 + c % 4
                    cx.mm(bPS[bk], [(bank(bk)[:, :W], w2b[slot][:, k, :], u_t[:, k, :W]) for k in range(32)], [bw2b[slot]] + bu)
                    cx.op(DVE, lambda e, c=c, bk=bk: e.scalar_tensor_tensor(out=xt[s][:, c, :W], in0=bank(bk)[:, :W], scalar=mod[:, 40 + c, j:j + 1],
                                                                             in1=xt[s][:, c, :W], op0=ALU.mult, op1=ALU.add),
                          [bPS[bk], bmod, bxt[s]], [bxt[s]])
                if not last:
                    cx.dma(SP, XS[:, t0:t0 + W].rearrange("(k p) t -> p k t", p=128), xt[s][:, :, :W], [bxt[s]], [bXS], bxt[s])
                else:
                    FO = L * VL
                    for c in range(8):
                        cx.op(DVE, lambda e, c=c: e.tensor_tensor(out=sqt[c % 2][:, :W], in0=xt[s][:, c, :W], in1=xt[s][:, c, :W], op=ALU.mult), [bxt[s]], [bsq[c % 2]])
                        cx.mm_acc(bPS[0], bank(0)[:, :W], onesDb[:, :], sqt[c % 2][:, :W], [bsq[c % 2], bconst], start=(c == 0), stop=(c == 7))
                    rstd_from(bank(0)[:, :W], bPS[0], rstd[:, :W], brstd, t32[0][:, :W], bt32[0])
                    for c in range(8):
                        cx.op(DVE, lambda e, c=c: e.scalar_tensor_tensor(out=xt[s][:, c, :W], in0=xt[s][:, c, :W], scalar=vecs[:, FO + c:FO + c + 1],
                                                                          in1=rstd[:, :W], op0=ALU.mult, op1=ALU.mult), [bxt[s], bvecs, brstd], [bxt[s]])
                    cx.dma(SP, outT[:, t0 - NCTX:t0 - NCTX + W].rearrange("(k p) t -> p k t", p=128), xt[s][:, :, :W], [bxt[s]], [Buf("o")], bxt[s])
            cx.barrier()

    cx.barrier()
    es_global.close()
    return nc


def _cols(v):
    v = np.asarray(v, np.float32)
    return np.ascontiguousarray(v.reshape(-1, 128).T)


def _rope_tables(S):
    NT = NCTX + S
    rows = S // GRID_W
    row = np.repeat(np.arange(rows, dtype=np.int32), GRID_W).astype(np.float32)
    col = np.tile(np.arange(GRID_W, dtype=np.int32), rows).astype(np.float32)
    n = 8
    inv = (1.0 / (np.float32(10000.0) ** (np.arange(n, dtype=np.float32) / n))).astype(np.float32)
    C = np.ones((32, NT), np.float32)
    Sg = np.zeros((32, NT), np.float32)
    for d in range(32):
        a, jj = d // 16, d % 16
        b_, i = jj // 8, jj % 8
        pos = row if a == 0 else col
        ang = (pos * inv[i]).astype(np.float32)
        C[d, NCTX:] = np.cos(ang)
        Sg[d, NCTX:] = (-1.0 if b_ == 0 else 1.0) * np.sin(ang)
    return C, Sg


_PROG_CACHE = {}


def kernel_impl(inputs, S, DEPTH, ncores):
    f = lambda k: np.asarray(inputs[k], np.float32)
    L = DEPTH
    x, c, ctx, c_ctx = f("x"), f("c"), f("ctx"), f("c_ctx")
    NT = NCTX + S
    ropeC, ropeS = _rope_tables(S)
    ident = np.eye(128, dtype=np.float32)
    ada_w = np.ascontiguousarray(f("ada_w")[:L])
    w_in = np.ascontiguousarray(f("w_in")[:L])
    w_uq = np.ascontiguousarray(f("w_uq")[:L])
    w_ukv = np.ascontiguousarray(f("w_ukv")[:L])
    w_out = np.ascontiguousarray(f("w_out")[:L])
    w_ff1 = np.ascontiguousarray(f("w_ff1")[:L])
    w_ff2 = np.ascontiguousarray(f("w_ff2")[:L])
    sgu_w = f("sgu_w")[:L]
    wsT = np.ascontiguousarray(sgu_w.transpose(0, 3, 1, 2))
    sgb = np.stack([np.broadcast_to(np.stack([f("sgu_ln_g")[l], f("sgu_ln_b")[l]], 0)[None], (128, 2, 256)) for l in range(L)], 0)
    sgb = np.ascontiguousarray(sgb, dtype=np.float32)
    sgu_b = f("sgu_b")[:L]
    bsT = np.zeros((L, 128, 2, 128), np.float32)
    for cc in range(2):
        for hh in range(2):
            bsT[:, hh * 64:(hh + 1) * 64, cc, :] = sgu_b[:, 2 * cc + hh, None, :]
    per_layer = []
    for l in range(L):
        cw = f("conv_w")[l]
        convw = np.concatenate([cw[:, cc * 128:(cc + 1) * 128].T for cc in range(2)], 1)
        per_layer.append(np.concatenate([
            _cols(f("norm1_g")[l]), _cols(f("norm2_g")[l]), _cols(f("q_norm_g")[l]), _cols(f("kv_norm_g")[l]),
            _cols(f("conv_b")[l]), _cols(f("conv_ln_g")[l]), _cols(f("conv_ln_b")[l]), _cols(f("ada_b")[l]), convw], 1))
    shared = {"ropeC": ropeC, "ropeS": ropeS, "ident": ident, "ada_w": ada_w, "w_in": w_in, "w_uq": w_uq, "w_ukv": w_ukv,
              "wsT": wsT, "sgb": sgb, "bsT": bsT, "w_out": w_out, "w_ff1": w_ff1, "w_ff2": w_ff2}
    in_maps = []
    for b in range(ncores):
        sv = np.stack([_cols(c[b]), _cols(c_ctx)], -1).reshape(128, 16)
        vecs = np.concatenate(per_layer + [_cols(f("final_g")), sv], 1).astype(np.float32)
        xT = np.ascontiguousarray(np.concatenate([ctx[b].T, x[b].T], 1))
        m = dict(shared)
        m["vecs"] = np.ascontiguousarray(vecs)
        m["xT"] = xT
        in_maps.append(m)
    key = (S, DEPTH)
    if key not in _PROG_CACHE:
        _PROG_CACHE[key] = build_program(S, DEPTH)
    nc = _PROG_CACHE[key]
    res = run_bass_kernel_spmd(nc, in_maps, core_ids=list(range(ncores)))
    out = np.stack([np.ascontiguousarray(r["outT"].T) for r in res.results], 0)
    return out.astype(np.float32)


def kernel(**inputs):
    return kernel_impl(inputs, 8192, 4, 8)
```

```python
import numpy as np
from contextlib import ExitStack
import concourse.bass as bass
import concourse.mybir as mybir
from concourse.bass_utils import run_bass_kernel_spmd

F32 = mybir.dt.float32
BF16 = mybir.dt.bfloat16
ALU = mybir.AluOpType
AF = mybir.ActivationFunctionType
AX = mybir.AxisListType

D = 1024
NCTX = 256
GRID_W = 64
HEADS = 8
DQK = 96
DV = 64
QR = 384
KVR = 256
D_IN = 1696
DFF = 4096
EPS = 1e-6
ATTN_SCALE = 96 ** -0.5
CW = 31
VL = 137
O_N1G, O_N2G, O_QNG, O_KVNG, O_CONVB, O_CLNG, O_CLNB, O_ADAB, O_CONVW = 0, 8, 16, 19, 21, 23, 25, 27, 75


class Buf:
    __slots__ = ("name", "w", "r", "accum", "dsem", "dcnt", "psum")

    def __init__(self, name, accum=False, psum=False):
        self.name = name
        self.w = {}
        self.r = {}
        self.accum = accum
        self.psum = psum
        self.dsem = None
        self.dcnt = 0


def _merge(d, s):
    for k, v in s.items():
        o = d.get(k)
        if o is None or o[1] < v[1]:
            d[k] = v


class Eng:
    def __init__(self, nc, e, name, compute=True):
        self.e = e
        self.name = name
        self.sem = nc.alloc_semaphore("s_" + name) if compute else None
        self.cnt = 0
        self.seen = {}

    def wait(self, deps):
        for k, (sem, val) in deps.items():
            if self.seen.get(k, 0) < val:
                self.e.wait_ge(sem, val)
                self.seen[k] = val


class Ctx:
    def __init__(self, nc):
        self.nc = nc
        self.PE = Eng(nc, nc.tensor, "pe")
        self.ACT = Eng(nc, nc.scalar, "act")
        self.DVE = Eng(nc, nc.vector, "dve")
        self.POOL = Eng(nc, nc.gpsimd, "pool")
        self.SP = Eng(nc, nc.sync, "sp", compute=False)
        self.dsems = {}

    def _deps(self, reads, writes, E=None):
        deps = {}
        for b in reads:
            _merge(deps, b.w)
            if b.psum:
                own = id(E.sem) if (E is not None and E.sem is not None) else None
                _merge(deps, {k: v for k, v in b.r.items() if k != own})
        for b in writes:
            _merge(deps, b.r)
            if not b.accum:
                _merge(deps, b.w)
        return deps

    def _record(self, tok, reads, writes):
        key = id(tok[0])
        for b in reads:
            o = b.r.get(key)
            if o is None or o[1] < tok[1]:
                b.r[key] = tok
        for b in writes:
            if b.accum:
                o = b.w.get(key)
                if o is None or o[1] < tok[1]:
                    b.w[key] = tok
            else:
                b.w = {key: tok}
                b.r = {}

    def op(self, E, fn, reads=(), writes=()):
        E.wait(self._deps(reads, writes, E))
        ins = fn(E.e)
        E.cnt += 1
        ins.then_inc(E.sem, 1)
        self._record((E.sem, E.cnt), reads, writes)

    def mm(self, ps_buf, mms, reads, last_reads=()):
        E = self.PE
        E.wait(self._deps(reads, [ps_buf]))
        n = len(mms)
        ins = None
        for i, (o, l, r) in enumerate(mms):
            ins = E.e.matmul(o, l, r, start=(i == 0), stop=(i == n - 1))
        E.cnt += 1
        ins.then_inc(E.sem, 1)
        self._record((E.sem, E.cnt), reads, [ps_buf])

    def mm_acc(self, ps_buf, o, l, r, reads, start, stop, mark=True):
        E = self.PE
        if not mark:
            deps = {}
            for b in reads:
                _merge(deps, b.w)
            if start:
                _merge(deps, ps_buf.r)
                _merge(deps, ps_buf.w)
            E.wait(deps)
            E.e.matmul(o, l, r, start=start, stop=stop)
            if start:
                ps_buf.r = {}
            return
        deps = {}
        for b in reads:
            _merge(deps, b.w)
        if start:
            _merge(deps, ps_buf.r)
            _merge(deps, ps_buf.w)
        E.wait(deps)
        ins = E.e.matmul(o, l, r, start=start, stop=stop)
        E.cnt += 1
        ins.then_inc(E.sem, 1)
        tok = (E.sem, E.cnt)
        key = id(tok[0])
        for b in reads:
            b.r[key] = tok
        if start:
            ps_buf.r = {}
        ps_buf.w = {key: tok}

    def dma(self, Q, out, in_, reads, writes, owner):
        Q.wait(self._deps(reads, writes))
        ent = self.dsems.get(owner.name)
        if ent is None:
            ent = [self.nc.alloc_semaphore("d_" + owner.name), 0]
            self.dsems[owner.name] = ent
        ins = Q.e.dma_start(out=out, in_=in_)
        ent[1] += 16
        ins.then_inc(ent[0], 16)
        self._record((ent[0], ent[1]), reads, writes)

    def barrier(self):
        deps = {}
        for E in (self.PE, self.ACT, self.DVE, self.POOL):
            if E.cnt:
                deps[id(E.sem)] = (E.sem, E.cnt)
        for ent in self.dsems.values():
            deps[id(ent[0])] = (ent[0], ent[1])
        for E in (self.PE, self.ACT, self.DVE, self.POOL, self.SP):
            E.wait(deps)


class _Stop(Exception):
    pass


def build_program(S, DEPTH, stop=None):
    try:
        return _build(S, DEPTH, stop)
    except _Stop as e:
        nc, cx, es_global = e.payload
        cx.barrier()
        return nc


def _build(S, DEPTH, stop=None):
    NT = NCTX + S
    NBLK = NT // 128
    GSW = NT + 64
    nc = bass.Bass("TRN2", target_bir_lowering=False)
    cx = Ctx(nc)
    PE, ACT, DVE, POOL, SP = cx.PE, cx.ACT, cx.DVE, cx.POOL, cx.SP
    L = DEPTH

    def din(name, shape, dt=F32):
        return nc.dram_tensor(name, list(shape), dt, kind="ExternalInput").ap()

    def dscr(name, shape, dt):
        return nc.dram_tensor(name, list(shape), dt).ap()

    xT_in = din("xT", [D, NT])
    vecs_in = din("vecs", [128, L * VL + 24])
    ropeC_in = din("ropeC", [32, NT])
    ropeS_in = din("ropeS", [32, NT])
    ident_in = din("ident", [128, 128])
    ada_w_in = din("ada_w", [L, D, 6 * D])
    w_in_in = din("w_in", [L, D, D_IN])
    w_uq_in = din("w_uq", [L, QR, HEADS * DQK])
    w_ukv_in = din("w_ukv", [L, KVR, HEADS * 128])
    wsT_in = din("wsT", [L, 128, 4, 128])
    sgb_in = din("sgb", [L, 128, 2, 256])
    bsT_in = din("bsT", [L, 128, 2, 128])
    w_out_in = din("w_out", [L, D, D])
    w_ff1_in = din("w_ff1", [L, D, DFF])
    w_ff2_in = din("w_ff2", [L, DFF, D])
    outT = nc.dram_tensor("outT", [D, S], F32, kind="ExternalOutput").ap()

    XS = dscr("XS", [D, NT], F32)
    QS = dscr("QS", [HEADS, 97, NT], BF16)
    KS = dscr("KS", [HEADS, 64, NT], BF16)
    KR = dscr("KR", [32, NT], BF16)
    KONE = dscr("KONE", [1, NT], BF16)
    VS = dscr("VS", [HEADS, 128, NBLK, 65], BF16)
    GS = dscr("GS", [256, GSW], BF16)
    YS = dscr("YS", [D, NT], BF16)
    QN2 = dscr("QN2", [8, NT], F32)
    WinS = dscr("WinS", [L, D, D_IN], BF16)
    WuqS = dscr("WuqS", [L, QR, HEADS * DQK], BF16)
    WukvS = dscr("WukvS", [L, KVR, HEADS * 128], BF16)
    WoutS = dscr("WoutS", [L, D, D], BF16)
    W1S = dscr("W1S", [L, 8, 128, 8, 512], BF16)
    W2S = dscr("W2S", [L, 8, 128, 32, 128], BF16)
    WsS = dscr("WsS", [L, 128, 4, 128], BF16)
    bXS, bQS, bKS, bKR, bKONE, bVS, bGS, bYS, bQN2 = [Buf(n, accum=True) for n in
                                                    ("XS", "QS", "KS", "KR", "KONE", "VS", "GS", "YS", "QN2")]
    bWs = [Buf(f"Ws{l}", accum=True) for l in range(L)]
    bWb = [Buf(f"Wb{l}", accum=True) for l in range(L)]

    es_global = ExitStack()

    uid = [0]

    def sb(es, name, shape, dt):
        uid[0] += 1
        return es.enter_context(nc.sbuf_tensor(f"t{uid[0]}_{name}", list(shape), dt))

    PSALL = es_global.enter_context(nc.psum_tensor("psall", [128, 4096], F32))
    bPS = [Buf(f"bank{i}", psum=True) for i in range(8)]

    def bank(i):
        return PSALL[:, i * 512:(i + 1) * 512]

    vecs = sb(es_global, "vecs", [128, L * VL + 24], F32)
    bvecs = Buf("vecs")
    ident = sb(es_global, "ident", [128, 128], F32)
    bident = Buf("ident")
    onesD = sb(es_global, "onesD", [128, 128], F32)
    ones384 = sb(es_global, "ones384", [128, 128], F32)
    onesDb = sb(es_global, "onesDb", [128, 128], BF16)
    ones256 = sb(es_global, "ones256", [128, 128], F32)
    ones1 = sb(es_global, "ones1", [128, 64], F32)
    ind = sb(es_global, "ind", [128, 8, 8], BF16)
    ones8 = sb(es_global, "ones8", [128, 8], BF16)
    epsc = sb(es_global, "epsc", [128, 1], F32)
    onec = sb(es_global, "onec", [128, 1], F32)
    onesb = sb(es_global, "onesb", [128, 1024], BF16)
    zerob = sb(es_global, "zerob", [128, 64], BF16)
    bconst = Buf("const")
    s_sb = sb(es_global, "s_sb", [128, 8, 2], F32)
    bs_sb = Buf("s_sb")
    mod = sb(es_global, "mod", [128, 48, 2], F32)
    bmod = Buf("mod")
    gm = sb(es_global, "gm", [128, 2, 8, 2], F32)
    bgm = Buf("gm")
    kmax2 = sb(es_global, "kmax2", [8, 1], F32)
    bkmax = Buf("kmax2")
    stage = [sb(es_global, f"stage{i}", [128, 4096], BF16) for i in range(2)]
    bstage = [Buf(f"stage{i}") for i in range(2)]

    tiles = [(0, NCTX, True)] + [(NCTX + 512 * i, 512, False) for i in range(S // 512)]

    cx.dma(SP, vecs[:, :], vecs_in, [], [bvecs], bvecs)
    cx.dma(SP, ident[:, :], ident_in, [], [bident], bident)
    cx.op(DVE, lambda e: e.memset(onesD[:, :], 1.0 / D), [], [bconst])
    cx.op(DVE, lambda e: e.memset(ones384[:, :], 1.0 / QR), [], [bconst])
    cx.op(DVE, lambda e: e.memset(onesDb[:, :], 1.0 / D), [], [bconst])
    cx.op(DVE, lambda e: e.memset(ones256[:, :], 1.0 / 256), [], [bconst])
    cx.op(DVE, lambda e: e.memset(ones1[:, :], 1.0), [], [bconst])
    cx.op(DVE, lambda e: e.memset(ind[:, :, :], 0.0), [], [bconst])
    for h in range(8):
        cx.op(DVE, lambda e, h=h: e.memset(ind[:, h, h:h + 1], 1.0), [], [bconst])
    cx.op(DVE, lambda e: e.memset(ones8[:, :], 1.0), [], [bconst])
    cx.op(DVE, lambda e: e.memset(epsc[:, :], EPS), [], [bconst])
    cx.op(DVE, lambda e: e.memset(onec[:, :], 1.0), [], [bconst])
    cx.op(DVE, lambda e: e.memset(onesb[:, :], 1.0), [], [bconst])
    cx.op(DVE, lambda e: e.memset(zerob[:, :], 0.0), [], [bconst])
    cx.op(DVE, lambda e: e.memset(kmax2[:, :], 0.0), [], [bkmax])
    for c0 in range(0, NT, 1024):
        w = min(1024, NT - c0)
        cx.dma(SP, KONE[0:1, c0:c0 + w], onesb[0:1, 0:w], [bconst], [bKONE], bconst)
    for c in range(2):
        for (c0, w) in ((0, 16), (16 + NCTX, 32), (48 + NT, 16)):
            cx.dma(SP, GS[c * 128:(c + 1) * 128, c0:c0 + w], zerob[:, 0:w], [bconst], [bGS], bconst)
    SO = L * VL + 8
    cview = vecs[:, SO:SO + 16]
    s2 = s_sb[:, :, :].rearrange("p k j -> p (k j)")
    cx.op(ACT, lambda e: e.activation(out=s2, in_=cview, func=AF.Exp, scale=-1.0), [bvecs], [bs_sb])
    cx.op(DVE, lambda e: e.tensor_scalar_add(out=s2, in0=s2, scalar1=1.0), [bs_sb], [bs_sb])
    cx.op(DVE, lambda e: e.reciprocal(out=s2, in_=s2), [bs_sb], [bs_sb])
    cx.op(DVE, lambda e: e.tensor_tensor(out=s2, in0=s2, in1=cview, op=ALU.mult), [bs_sb, bvecs], [bs_sb])

    pieces = []
    for l in range(L):
        for k0 in range(0, 8, 2):
            pieces.append((w_in_in[l, k0 * 128:(k0 + 2) * 128, :].rearrange("(k p) n -> p k n", p=128),
                           WinS[l, k0 * 128:(k0 + 2) * 128, :].rearrange("(k p) n -> p k n", p=128), (2, D_IN), bWs[l]))
        pieces.append((w_uq_in[l].rearrange("(k p) n -> p k n", p=128),
                       WuqS[l].rearrange("(k p) n -> p k n", p=128), (3, 768), bWs[l]))
        pieces.append((w_ukv_in[l].rearrange("(k p) n -> p k n", p=128),
                       WukvS[l].rearrange("(k p) n -> p k n", p=128), (2, 1024), bWs[l]))
        pieces.append((wsT_in[l], WsS[l], (4, 128), bWs[l]))
    for l in range(L):
        for k0 in range(0, 8, 4):
            pieces.append((w_out_in[l, k0 * 128:(k0 + 4) * 128, :].rearrange("(k p) n -> p k n", p=128),
                           WoutS[l, k0 * 128:(k0 + 4) * 128, :].rearrange("(k p) n -> p k n", p=128), (4, 1024), bWb[l]))
        for g in range(8):
            pieces.append((w_ff1_in[l, :, g * 512:(g + 1) * 512].rearrange("(k p) n -> p k n", p=128),
                           W1S[l, g], (8, 512), bWb[l]))
        for k0 in range(0, 32, 4):
            pieces.append((w_ff2_in[l, k0 * 128:(k0 + 4) * 128, :].rearrange("(k p) (c n) -> p k c n", p=128, n=128),
                           W2S[l, :, :, k0:k0 + 4, :].rearrange("c p k n -> p k c n"), (4, 8, 128), bWb[l]))

    def stage_view(i, shp):
        n = int(np.prod(shp))
        v = stage[i % 2][:, 0:n]
        if len(shp) == 2:
            return v.rearrange("p (a b) -> p a b", a=shp[0])
        return v.rearrange("p (a b c) -> p a b c", a=shp[0], b=shp[1])

    def conv_load(i):
        src, dst, shp, bw_ = pieces[i]
        cx.dma(POOL, stage_view(i, shp), src, [], [bstage[i % 2]], bstage[i % 2])

    def conv_store(i):
        src, dst, shp, bw_ = pieces[i]
        if len(shp) == 3:
            sv = stage_view(i, shp)
            for c in range(shp[1]):
                cx.dma(POOL, dst[:, :, c, :], sv[:, :, c, :], [bstage[i % 2]], [bw_], bstage[i % 2])
        else:
            cx.dma(POOL, dst, stage_view(i, shp), [bstage[i % 2]], [bw_], bstage[i % 2])

    conv_load(0)
    for i in range(len(pieces)):
        if i + 1 < len(pieces):
            conv_load(i + 1)
        conv_store(i)

    def chk(name):
        if stop == name:
            e = _Stop()
            e.payload = (nc, cx, es_global)
            raise e

    chk("pro")
    for l in range(L):
        last = (l == L - 1)
        VB = l * VL
        Xsrc = xT_in if l == 0 else XS
        bXsrc = Buf("xin") if l == 0 else bXS

        with ExitStack() as es:
            adw = [sb(es, f"adw{i}", [128, 8, 512], F32) for i in range(2)]
            badw = [Buf(f"adw{i}") for i in range(2)]
            for g in range(12):
                cx.dma(SP, adw[g % 2][:, :, :],
                       ada_w_in[l, :, g * 512:(g + 1) * 512].rearrange("(k p) n -> p k n", p=128),
                       [], [badw[g % 2]], badw[g % 2])
                for jj in range(4):
                    j = g * 4 + jj
                    cx.mm(bPS[j % 2],
                          [(bank(j % 2)[:, 0:2], adw[g % 2][:, k, jj * 128:(jj + 1) * 128], s_sb[:, k, :]) for k in range(8)],
                          [badw[g % 2], bs_sb])
                    cx.op(DVE, lambda e, j=j: e.tensor_scalar(out=mod[:, j, :], in0=bank(j % 2)[:, 0:2],
                                                              scalar1=vecs[:, VB + O_ADAB + j:VB + O_ADAB + j + 1],
                                                              scalar2=None, op0=ALU.add),
                          [bPS[j % 2], bvecs], [bmod])
            for n, (og, osc) in enumerate(((O_N1G, 8), (O_N2G, 32))):
                for j in range(2):
                    cx.op(DVE, lambda e, n=n, j=j, osc=osc: e.tensor_scalar_add(out=gm[:, n, :, j], in0=mod[:, osc:osc + 8, j], scalar1=1.0),
                          [bmod], [bgm])
                    cx.op(DVE, lambda e, n=n, j=j, og=og: e.tensor_tensor(out=gm[:, n, :, j], in0=gm[:, n, :, j],
                                                                          in1=vecs[:, VB + og:VB + og + 8], op=ALU.mult),
                          [bgm, bvecs], [bgm])
            cx.barrier()
        chk(f"P0_{l}")

        def rstd_from(E_ps_ap, ps_buf, out_ap, out_buf, tmp_ap, tmp_buf):
            cx.op(ACT, lambda e: e.activation(out=tmp_ap, in_=E_ps_ap, func=AF.Ln, bias=epsc[:, :], scale=1.0),
                  [ps_buf, bconst], [tmp_buf])
            cx.op(ACT, lambda e: e.activation(out=out_ap, in_=tmp_ap, func=AF.Exp, scale=-0.5), [tmp_buf], [out_buf])

        def adanorm(es_tag, xt_t, bxt, W, n, j, h_t, bh, sqt, bsq, rstd, brstd, t32, bt32, statbank, sq_act=True, ones_mat=None):
            if ones_mat is None:
                ones_mat = onesD
            osh = 0 if n == 0 else 24
            for c in range(8):
                if sq_act:
                    cx.op(ACT, lambda e, c=c: e.activation(out=sqt[c % 2][:, :W], in_=xt_t[:, c, :W], func=AF.Square),
                          [bxt], [bsq[c % 2]])
                else:
                    cx.op(DVE, lambda e, c=c: e.tensor_tensor(out=sqt[c % 2][:, :W], in0=xt_t[:, c, :W], in1=xt_t[:, c, :W], op=ALU.mult),
                          [bxt], [bsq[c % 2]])
                cx.mm_acc(bPS[statbank], bank(statbank)[:, :W], ones_mat[:, :], sqt[c % 2][:, :W], [bsq[c % 2], bconst],
                          start=(c == 0), stop=(c == 7))
            rstd_from(bank(statbank)[:, :W], bPS[statbank], rstd[:, :W], brstd, t32[0][:, :W], bt32[0])
            for c in range(8):
                cx.op(DVE, lambda e, c=c: e.scalar_tensor_tensor(out=t32[c % 2][:, :W], in0=xt_t[:, c, :W],
                                                                  scalar=gm[:, n, c, j:j + 1], in1=rstd[:, :W],
                                                                  op0=ALU.mult, op1=ALU.mult),
                      [bxt, bgm, brstd], [bt32[c % 2]])
                cx.op(ACT, lambda e, c=c: e.activation(out=h_t[:, c, :W], in_=t32[c % 2][:, :W], func=AF.Identity,
                                                        bias=mod[:, osh + c, j:j + 1], scale=1.0),
                      [bt32[c % 2], bmod], [bh])

        with ExitStack() as es:
            w_in_sb = sb(es, "w_in_sb", [128, 8, D_IN + 32], BF16)
            w_uq_sb = sb(es, "w_uq_sb", [128, 3, 8, 128], BF16)
            w_ukv_sb = sb(es, "w_ukv_sb", [128, 2, 1024], BF16)
            wsT_sb = sb(es, "wsT_sb", [128, 4, 128], BF16)
            sgb_sb = sb(es, "sgb_sb", [128, 2, 256], F32)
            bsT_sb = sb(es, "bsT_sb", [128, 2, 128], F32)
            bw1 = Buf("w1")
            xt = [sb(es, f"xt{i}", [128, 8, 512], F32) for i in range(2)]
            bxt = [Buf(f"xt{i}") for i in range(2)]
            ropet = [sb(es, f"ropet{i}", [128, 2, 512], F32) for i in range(2)]
            bropet = [Buf(f"ropet{i}") for i in range(2)]
            sqt = [sb(es, f"sqt{i}", [128, 512], F32) for i in range(2)]
            bsq = [Buf(f"sq{i}") for i in range(2)]
            t32 = [sb(es, f"t32_{i}", [128, 512], F32) for i in range(4)]
            bt32 = [Buf(f"t32_{i}") for i in range(4)]
            rstd = sb(es, "rstd", [128, 512], F32)
            brstd = Buf("rstd")
            h_t2 = [sb(es, f"h_t{i}", [128, 8, 512], BF16) for i in range(2)]
            bh2 = [Buf(f"h{i}") for i in range(2)]
            sqtA = [sb(es, f"sqtA{i}", [128, 512], BF16) for i in range(2)]
            bsqA = [Buf(f"sqA{i}") for i in range(2)]
            t32A = [sb(es, f"t32A{i}", [128, 512], F32) for i in range(2)]
            bt32A = [Buf(f"t32A{i}") for i in range(2)]
            rstdA = sb(es, "rstdA", [128, 512], F32)
            brstdA = Buf("rstdA")
            zqn = sb(es, "zqn", [128, 3, 512], BF16)
            bzqn = Buf("zqn")
            kvn = sb(es, "kvn", [128, 2, 512], BF16)
            bkvn = Buf("kvn")
            qt = [sb(es, f"qt{i}", [128, 512], BF16) for i in range(2)]
            bqt = [Buf(f"qt{i}") for i in range(2)]
            sqq = [sb(es, f"sqq{i}", [128, 512], BF16) for i in range(2)]
            bsqq = [Buf(f"sqq{i}") for i in range(2)]
            kt = [sb(es, f"kt{i}", [128, 512], BF16) for i in range(2)]
            bkt = [Buf(f"kt{i}") for i in range(2)]
            krt = sb(es, "krt", [128, 512], BF16)
            bkrt = Buf("krt")
            vt = sb(es, "vt", [128, 8, 4, 65], BF16)
            bvt = Buf("vt")
            glt = sb(es, "glt", [128, 2, 512], BF16)
            bglt = Buf("glt")
            ut = sb(es, "ut", [128, 2, 512], BF16)
            but = Buf("ut")
            vnt = [sb(es, f"vnt{i}", [128, 256], BF16) for i in range(2)]
            bvnt = [Buf(f"vnt{i}") for i in range(2)]
            sgo = sb(es, "sgo", [128, 2, 512], BF16)
            bsgo = Buf("sgo")
            qn_sb = sb(es, "qn_sb", [8, 512], F32)
            bqn = Buf("qn_sb")
            kn_sb = sb(es, "kn_sb", [8, 1], F32)
            bkn = Buf("kn_sb")
            tv = sb(es, "tv", [128, 4, 3, 256], F32)
            btA = [Buf(f"tA{i}") for i in range(4)]
            btB = [Buf(f"tB{i}") for i in range(4)]
            btV = [Buf(f"tV{i}") for i in range(4)]
            stv = sb(es, "stv", [128, 4, 8], F32)
            bstv = [Buf(f"stv{i}") for i in range(4)]
            vnt4 = sb(es, "vnt4", [128, 4, 256], BF16)
            bvnt4 = [Buf(f"vnt4_{i}") for i in range(4)]

            cx.dma(SP, w_in_sb[:, :, 0:D_IN], WinS[l].rearrange("(k p) n -> p k n", p=128), [bWs[l]], [bw1], bw1)
            for k in range(3):
                cx.dma(SP, w_uq_sb[:, k, :, 0:96], WuqS[l, k * 128:(k + 1) * 128, :].rearrange("p (h d) -> p h d", d=96), [bWs[l]], [bw1], bw1)
            cx.dma(SP, w_ukv_sb[:, :, :], WukvS[l].rearrange("(k p) n -> p k n", p=128), [bWs[l]], [bw1], bw1)
            cx.dma(SP, wsT_sb[:, :, :], WsS[l], [bWs[l]], [bw1], bw1)
            cx.dma(SP, sgb_sb[:, :, :], sgb_in[l], [], [bw1], bw1)
            cx.dma(SP, bsT_sb[:, :, :], bsT_in[l], [], [bw1], bw1)
            cx.op(DVE, lambda e: e.memset(vt[:, :, :, 64:65], 1.0), [], [bvt])
            cx.op(DVE, lambda e: e.memset(kmax2[:, :], 0.0), [bkmax], [bkmax])
            for k in range(8):
                src = w_in_sb[:, k, 640:672].rearrange("p (a b j) -> p a b j", a=2, b=2)
                dst = w_in_sb[:, k, D_IN:D_IN + 32].rearrange("p (a b j) -> p a b j", a=2, b=2)
                for b_ in range(2):
                    cx.op(DVE, lambda e, src=src, dst=dst, b_=b_: e.tensor_copy(out=dst[:, :, b_, :], in_=src[:, :, 1 - b_, :]), [bw1], [bw1])
            for k in range(3):
                for b_ in range(2):
                    src = w_uq_sb[:, k, :, 64:96].rearrange("p h (a b j) -> p h a b j", a=2, b=2)
                    dst = w_uq_sb[:, k, :, 96:128].rearrange("p h (a b j) -> p h a b j", a=2, b=2)
                    for a_ in range(2):
                        cx.op(DVE, lambda e, src=src, dst=dst, b_=b_, a_=a_: e.tensor_copy(out=dst[:, :, a_, b_, :], in_=src[:, :, a_, 1 - b_, :]),
                              [bw1], [bw1])

            def load_x(ti):
                t0, W, isc = tiles[ti]
                s = ti % 2
                cx.dma(SP, xt[s][:, :, :W], Xsrc[:, t0:t0 + W].rearrange("(k p) t -> p k t", p=128), [bXsrc], [bxt[s]], bxt[s])

            def load_rope(ti):
                t0, W, isc = tiles[ti]
                s = ti % 2
                cx.dma(SP, ropet[s][64:96, 0, :W], ropeC_in[:, t0:t0 + W], [], [bropet[s]], bropet[s])
                cx.dma(SP, ropet[s][64:96, 1, :W], ropeS_in[:, t0:t0 + W], [], [bropet[s]], bropet[s])

            def gelu_from_psum(ps_ap, ps_buf, out_ap, out_buf, shape_sel):
                a, b_ = (t32[1], bt32[1]), (t32[2], bt32[2])
                A = shape_sel(a[0]); B = shape_sel(b_[0])
                cx.op(ACT, lambda e: e.activation(out=A, in_=ps_ap, func=AF.Square, scale=0.21145921), [ps_buf], [a[1]])
                cx.op(DVE, lambda e: e.scalar_tensor_tensor(out=A, in0=A, scalar=1.0, in1=ps_ap, op0=ALU.add, op1=ALU.mult), [a[1], ps_buf], [a[1]])
                cx.op(ACT, lambda e: e.activation(out=B, in_=A, func=AF.Exp, scale=-2.0 * 0.7978845608028654), [a[1]], [b_[1]])
                cx.op(ACT, lambda e: e.activation(out=B, in_=B, func=AF.Identity, bias=onec[:, :], scale=1.0), [b_[1], bconst], [b_[1]])
                cx.op(DVE, lambda e: e.reciprocal(out=B, in_=B), [b_[1]], [b_[1]])
                cx.op(DVE, lambda e: e.tensor_tensor(out=out_ap, in0=B, in1=ps_ap, op=ALU.mult), [b_[1], ps_buf], [out_buf])

            def stageA(ti):
                t0, W, isc = tiles[ti]
                s = ti % 2
                j = 1 if isc else 0
                adanorm(es, xt[s], bxt[s], W, 0, j, h_t2[s], bh2[s], sqtA, bsqA, rstdA, brstdA, t32A, bt32A, 0, ones_mat=onesDb)

            def tile_funcs(ti):
                t0, W, isc = tiles[ti]
                s = ti % 2
                j = 1 if isc else 0
                nb = W // 128
                h_t = h_t2[s]
                bh = bh2[s]

                def proj(bk, col0, M, po=0):
                    cx.mm(bPS[bk], [(bank(bk)[po:po + M, :W], w_in_sb[:, k, col0:col0 + M], h_t[:, k, :W]) for k in range(8)], [bw1, bh])

                def B1():
                    for c in range(3):
                        proj(2 + c, c * 128, 128)
                    for c in range(3):
                        cx.op(ACT, lambda e, c=c: e.activation(out=sqt[c % 2][:, :W], in_=bank(2 + c)[:, :W], func=AF.Square), [bPS[2 + c]], [bsq[c % 2]])
                        cx.mm_acc(bPS[0], bank(0)[:, :W], ones384[:, :], sqt[c % 2][:, :W], [bsq[c % 2], bconst], start=(c == 0), stop=(c == 2))
                    rstd_from(bank(0)[:, :W], bPS[0], rstd[:, :W], brstd, t32[0][:, :W], bt32[0])
                    for c in range(3):
                        cx.op(DVE, lambda e, c=c: e.scalar_tensor_tensor(out=zqn[:, c, :W], in0=bank(2 + c)[:, :W],
                                                                          scalar=vecs[:, VB + O_QNG + c:VB + O_QNG + c + 1], in1=rstd[:, :W],
                                                                          op0=ALU.mult, op1=ALU.mult),
                              [bPS[2 + c], bvecs, brstd], [bzqn])
                    chk(f"q1_{l}_{ti}")
                    for hh in range(HEADS):
                        ba, bb = 2 + 2 * (hh % 2), 3 + 2 * (hh % 2)
                        qs = hh % 2
                        cx.mm(bPS[ba], [(bank(ba)[0:96, :W], w_uq_sb[:, k, hh, 0:96], zqn[:, k, :W]) for k in range(3)], [bw1, bzqn])
                        cx.mm(bPS[bb], [(bank(bb)[64:96, :W], w_uq_sb[:, k, hh, 96:128], zqn[:, k, :W]) for k in range(3)], [bw1, bzqn])
                        cx.op(ACT, lambda e, ba=ba, qs=qs: e.activation(out=qt[qs][0:64, :W], in_=bank(ba)[0:64, :W], func=AF.Copy), [bPS[ba]], [bqt[qs]])
                        cx.op(ACT, lambda e, ba=ba, qs=qs: e.activation(out=sqq[qs][0:96, :W], in_=bank(ba)[0:96, :W], func=AF.Square), [bPS[ba]], [bsqq[qs]])
                        chk(f"q2_{l}_{ti}")
                        cx.op(DVE, lambda e, ba=ba: e.tensor_tensor(out=t32[0][64:96, :W], in0=bank(ba)[64:96, :W], in1=ropet[s][64:96, 0, :W], op=ALU.mult),
                              [bPS[ba], bropet[s]], [bt32[0]])
                        chk(f"q2a_{l}_{ti}")
                        cx.op(DVE, lambda e, bb=bb: e.tensor_tensor(out=t32[1][64:96, :W], in0=bank(bb)[64:96, :W], in1=ropet[s][64:96, 1, :W], op=ALU.mult),
                              [bPS[bb], bropet[s]], [bt32[1]])
                        chk(f"q2b_{l}_{ti}")
                        cx.op(DVE, lambda e, qs=qs: e.tensor_tensor(out=qt[qs][64:96, :W], in0=t32[0][64:96, :W], in1=t32[1][64:96, :W], op=ALU.add),
                              [bt32[0], bt32[1]], [bqt[qs]])
                        chk(f"q3_{l}_{ti}")
                        cx.mm_acc(bPS[1], bank(1)[0:8, :W], ind[0:96, hh, :], sqq[qs][0:96, :W], [bsqq[qs], bconst], start=(hh == 0), stop=(hh == 7))
                        chk(f"q4_{l}_{ti}")
                        cx.dma(SP, QS[hh, 0:96, t0:t0 + W], qt[qs][0:96, :W], [bqt[qs]], [bQS], bqt[qs])
                    cx.op(DVE, lambda e: e.tensor_copy(out=qn_sb[:, :W], in_=bank(1)[0:8, :W]), [bPS[1]], [bqn])
                    cx.dma(SP, QN2[:, t0:t0 + W], qn_sb[:, :W], [bqn], [bQN2], bqn)

                    chk(f"P1b_{l}_{ti}")
                    for c in range(2):
                        proj(6 + c, QR + c * 128, 128)
                    for c in range(2):
                        cx.op(ACT, lambda e, c=c: e.activation(out=sqt[c % 2][:, :W], in_=bank(6 + c)[:, :W], func=AF.Square), [bPS[6 + c]], [bsq[c % 2]])
                        cx.mm_acc(bPS[0], bank(0)[:, :W], ones256[:, :], sqt[c % 2][:, :W], [bsq[c % 2], bconst], start=(c == 0), stop=(c == 1))
                    rstd_from(bank(0)[:, :W], bPS[0], rstd[:, :W], brstd, t32[0][:, :W], bt32[0])
                    for c in range(2):
                        cx.op(DVE, lambda e, c=c: e.scalar_tensor_tensor(out=kvn[:, c, :W], in0=bank(6 + c)[:, :W],
                                                                          scalar=vecs[:, VB + O_KVNG + c:VB + O_KVNG + c + 1], in1=rstd[:, :W],
                                                                          op0=ALU.mult, op1=ALU.mult),
                              [bPS[6 + c], bvecs, brstd], [bkvn])
                    proj(6, 640, 32, po=64)
                    proj(7, D_IN, 32, po=64)
                    cx.op(DVE, lambda e: e.tensor_tensor(out=t32[0][64:96, :W], in0=bank(6)[64:96, :W], in1=ropet[s][64:96, 0, :W], op=ALU.mult),
                          [bPS[6], bropet[s]], [bt32[0]])
                    cx.op(DVE, lambda e: e.tensor_tensor(out=t32[1][64:96, :W], in0=bank(7)[64:96, :W], in1=ropet[s][64:96, 1, :W], op=ALU.mult),
                          [bPS[7], bropet[s]], [bt32[1]])
                    cx.op(DVE, lambda e: e.tensor_tensor(out=krt[64:96, :W], in0=t32[0][64:96, :W], in1=t32[1][64:96, :W], op=ALU.add),
                          [bt32[0], bt32[1]], [bkrt])
                    cx.dma(SP, KR[:, t0:t0 + W], krt[64:96, :W], [bkrt], [bKR], bkrt)
                    cx.op(ACT, lambda e: e.activation(out=sqq[0][64:96, :W], in_=krt[64:96, :W], func=AF.Square), [bkrt], [bsqq[0]])
                    cx.mm_acc(bPS[1], bank(1)[0:8, :W], ones8[64:96, :], sqq[0][64:96, :W], [bsqq[0], bconst], start=True, stop=False)
                    for hh in range(HEADS):
                        bk = 2 + (hh % 4)
                        ks = hh % 2
                        cx.mm(bPS[bk], [(bank(bk)[0:64, :W], w_ukv_sb[:, k, hh * 128:hh * 128 + 64], kvn[:, k, :W]) for k in range(2)], [bw1, bkvn])
                        cx.op(ACT, lambda e, bk=bk, ks=ks: e.activation(out=kt[ks][0:64, :W], in_=bank(bk)[0:64, :W], func=AF.Copy), [bPS[bk]], [bkt[ks]])
                        cx.op(ACT, lambda e, bk=bk, ks=ks: e.activation(out=sqq[1][0:64, :W], in_=bank(bk)[0:64, :W], func=AF.Square), [bPS[bk]], [bsqq[1]])
                        cx.mm_acc(bPS[1], bank(1)[0:8, :W], ind[0:64, hh, :], sqq[1][0:64, :W], [bsqq[1], bconst], start=False, stop=(hh == 7))
                        cx.dma(SP, KS[hh, :, t0:t0 + W], kt[ks][0:64, :W], [bkt[ks]], [bKS], bkt[ks])
                    cx.op(DVE, lambda e: e.tensor_reduce(out=kn_sb[:, :], in_=bank(1)[0:8, :W], axis=AX.X, op=ALU.max), [bPS[1]], [bkn])
                    cx.op(DVE, lambda e: e.tensor_tensor(out=kmax2[:, :], in0=kmax2[:, :], in1=kn_sb[:, :], op=ALU.max), [bkn, bkmax], [bkmax])
                    chk(f"P1c_{l}_{ti}")
                    nb = W // 128
                    for b_ in range(nb):
                        bk = 2 + b_
                        cx.mm(bPS[bk], [(bank(bk)[:, 0:512].rearrange("p (h d) -> p h d", h=8),
                                         kvn[:, k, b_ * 128:(b_ + 1) * 128],
                                         w_ukv_sb[:, k, :].rearrange("p (h c) -> p h c", c=128)[:, :, 64:128]) for k in range(2)], [bw1, bkvn])
                        cx.op(ACT, lambda e, bk=bk, b_=b_: e.activation(out=vt[:, :, b_, 0:64], in_=bank(bk)[:, 0:512].rearrange("p (h d) -> p h d", h=8), func=AF.Copy),
                              [bPS[bk]], [bvt])
                    blk0 = t0 // 128
                    for hh in range(HEADS):
                        cx.dma(SP, VS[hh, :, blk0:blk0 + nb, :], vt[:, hh, 0:nb, :], [bvt], [bVS], bvt)


                def CONV():
                    chk(f"P1d_{l}_{ti}")
                    for c in range(4):
                        proj(2 + c, 672 + c * 128, 128)
                    for c in range(2):
                        A = t32[2 + c][:, :W]
                        cx.op(ACT, lambda e, c=c, A=A: e.activation(out=A, in_=bank(4 + c)[:, :W], func=AF.Exp, scale=-1.0), [bPS[4 + c]], [bt32[2 + c]])
                        cx.op(ACT, lambda e, A=A: e.activation(out=A, in_=A, func=AF.Identity, bias=onec[:, :], scale=1.0), [bt32[2 + c], bconst], [bt32[2 + c]])
                        cx.op(DVE, lambda e, A=A: e.reciprocal(out=A, in_=A), [bt32[2 + c]], [bt32[2 + c]])
                        cx.op(DVE, lambda e, c=c, A=A: e.tensor_tensor(out=glt[:, c, :W], in0=A, in1=bank(2 + c)[:, :W], op=ALU.mult),
                              [bt32[2 + c], bPS[2 + c]], [bglt])
                    gcol = (16 if isc else 48) + t0
                    cx.dma(SP, GS[:, gcol:gcol + W].rearrange("(c p) t -> p c t", p=128), glt[:, :, :W], [bglt], [bGS], bglt)


                def SGU():
                    chk(f"P1e_{l}_{ti}")
                    for c in range(2):
                        proj(6 + c, 1184 + c * 128, 128)
                        gelu_from_psum(bank(6 + c)[:, :W], bPS[6 + c], ut[:, c, :W], but, lambda t: t[:, :W])
                    K2 = -2.0 * 0.7978845608028654
                    R = range(nb)
                    for b_ in R:
                        cx.mm(bPS[2 + b_], [(bank(2 + b_)[:, 0:256], h_t[:, k, b_ * 128:(b_ + 1) * 128], w_in_sb[:, k, 1440:1696]) for k in range(8)], [bw1, bh])
                    PSV = [bank(2 + b_)[:, 0:256] for b_ in R]
                    TA = [tv[:, b_, 0, :] for b_ in R]
                    TB = [tv[:, b_, 1, :] for b_ in R]
                    TV = [tv[:, b_, 2, :] for b_ in R]
                    for b_ in R:
                        cx.op(ACT, lambda e, b_=b_: e.activation(out=TA[b_], in_=PSV[b_], func=AF.Square, scale=0.21145921), [bPS[2 + b_]], [btA[b_]])
                    for b_ in R:
                        cx.op(DVE, lambda e, b_=b_: e.scalar_tensor_tensor(out=TA[b_], in0=TA[b_], scalar=1.0, in1=PSV[b_], op0=ALU.add, op1=ALU.mult),
                              [btA[b_], bPS[2 + b_]], [btA[b_]])
                    for b_ in R:
                        cx.op(ACT, lambda e, b_=b_: e.activation(out=TB[b_], in_=TA[b_], func=AF.Exp, scale=K2), [btA[b_]], [btB[b_]])
                    for b_ in R:
                        cx.op(ACT, lambda e, b_=b_: e.activation(out=TB[b_], in_=TB[b_], func=AF.Identity, bias=onec[:, :], scale=1.0), [btB[b_], bconst], [btB[b_]])
                    for b_ in R:
                        cx.op(DVE, lambda e, b_=b_: e.reciprocal(out=TB[b_], in_=TB[b_]), [btB[b_]], [btB[b_]])
                    for b_ in R:
                        cx.op(DVE, lambda e, b_=b_: e.scalar_tensor_tensor(out=TV[b_], in0=TB[b_], scalar=1.0, in1=PSV[b_], op0=ALU.mult, op1=ALU.mult,
                                                                            accum_out=stv[:, b_, 0:1]), [btB[b_], bPS[2 + b_]], [btV[b_], bstv[b_]])
                    for b_ in R:
                        cx.op(ACT, lambda e, b_=b_: e.activation(out=TA[b_], in_=TV[b_], func=AF.Square, accum_out=stv[:, b_, 2:3]), [btV[b_]], [btA[b_], bstv[b_]])
                    allst = [bstv[b_] for b_ in R]
                    cx.op(DVE, lambda e: e.tensor_scalar(out=stv[:, 0:nb, 1], in0=stv[:, 0:nb, 0], scalar1=1.0 / 256, scalar2=None, op0=ALU.mult), allst, allst)
                    cx.op(DVE, lambda e: e.tensor_tensor(out=stv[:, 0:nb, 5], in0=stv[:, 0:nb, 1], in1=stv[:, 0:nb, 1], op=ALU.mult), allst, allst)
                    cx.op(DVE, lambda e: e.scalar_tensor_tensor(out=stv[:, 0:nb, 3], in0=stv[:, 0:nb, 2], scalar=1.0 / 256, in1=stv[:, 0:nb, 5],
                                                                 op0=ALU.mult, op1=ALU.subtract), allst, allst)
                    cx.op(ACT, lambda e: e.activation(out=stv[:, 0:nb, 3], in_=stv[:, 0:nb, 3], func=AF.Ln, bias=epsc[:, :], scale=1.0), allst + [bconst], allst)
                    cx.op(ACT, lambda e: e.activation(out=stv[:, 0:nb, 4], in_=stv[:, 0:nb, 3], func=AF.Exp, scale=-0.5), allst, allst)
                    for b_ in R:
                        cx.op(DVE, lambda e, b_=b_: e.tensor_scalar(out=TV[b_], in0=TV[b_], scalar1=stv[:, b_, 1:2], scalar2=stv[:, b_, 4:5],
                                                                    op0=ALU.subtract, op1=ALU.mult), [btV[b_], bstv[b_]], [btV[b_]])
                    for b_ in R:
                        cx.op(DVE, lambda e, b_=b_: e.tensor_tensor(out=TV[b_], in0=TV[b_], in1=sgb_sb[:, 0, :], op=ALU.mult), [btV[b_], bw1], [btV[b_]])
                    for b_ in R:
                        cx.op(DVE, lambda e, b_=b_: e.tensor_tensor(out=vnt4[:, b_, :], in0=TV[b_], in1=sgb_sb[:, 1, :], op=ALU.add), [btV[b_], bw1], [bvnt4[b_]])

                    def tail():
                        for b_ in R:
                            for cc in range(2):
                                mb = 6 + cc
                                for hh2 in range(2):
                                    hd = 2 * cc + hh2
                                    if hh2 == 0:
                                        cx.mm(bPS[mb], [(bank(mb)[0:64, 0:128], vnt4[:, b_, hd * 64:(hd + 1) * 64], wsT_sb[:, hd, :])], [bvnt4[b_], bw1])
                                    else:
                                        PE.wait(cx._deps([bvnt4[b_], bw1], []))
                                        ins = PE.e.matmul(bank(mb)[64:128, 0:128], vnt4[:, b_, hd * 64:(hd + 1) * 64], wsT_sb[:, hd, :], start=True, stop=True)
                                        PE.cnt += 1
                                        ins.then_inc(PE.sem, 1)
                                        tok = (PE.sem, PE.cnt)
                                        bPS[mb].w = {id(PE.sem): tok}
                                        bvnt4[b_].r[id(PE.sem)] = tok
                                cx.op(DVE, lambda e, mb=mb, cc=cc, b_=b_: e.tensor_tensor(out=TB[b_][:, 0:128], in0=bank(mb)[:, 0:128], in1=bsT_sb[:, cc, :], op=ALU.add),
                                      [bPS[mb], bw1], [btB[b_]])
                                cx.op(DVE, lambda e, cc=cc, b_=b_: e.tensor_tensor(out=sgo[:, cc, b_ * 128:(b_ + 1) * 128], in0=TB[b_][:, 0:128],
                                                                                    in1=ut[:, cc, b_ * 128:(b_ + 1) * 128], op=ALU.mult),
                                      [btB[b_], but], [bsgo])
                        cx.dma(SP, YS[768:1024, t0:t0 + W].rearrange("(c p) t -> p c t", p=128), sgo[:, :, :W], [bsgo], [bYS], bsgo)
                    return tail

                return B1, CONV, SGU

            NTL = len(tiles)
            load_x(0)
            load_rope(0)
            if NTL > 1:
                load_x(1)
            stageA(0)
            tail_ = None
            for ti in range(NTL):
                if ti + 1 < NTL:
                    load_rope(ti + 1)
                fB1, fCONV, fSGU = tile_funcs(ti)
                fB1()
                if tail_ is not None:
                    tail_()
                    tail_ = None
                fCONV()
                if ti + 1 < NTL:
                    stageA(ti + 1)
                    if ti + 2 < NTL:
                        load_x(ti + 2)
                tail_ = fSGU()
            if tail_ is not None:
                tail_()
            cx.barrier()
        chk(f"P1_{l}")

        es_mid = ExitStack()
        es = es_mid
        if True:
            CH = 2048
            fq = [sb(es, f"fq{i}", [8, CH], F32) for i in range(2)]
            bfq = [Buf(f"fq{i}") for i in range(2)]
            fb = [sb(es, f"fb{i}", [8, CH], BF16) for i in range(2)]
            bfb = [Buf(f"fb{i}") for i in range(2)]
            for ci, c0 in enumerate(range(0, NT, CH)):
                w = min(CH, NT - c0)
                s = ci % 2
                cx.dma(SP, fq[s][:, :w], QN2[:, c0:c0 + w], [bQN2], [bfq[s]], bfq[s])
                cx.op(DVE, lambda e, s=s, w=w: e.tensor_scalar(out=fq[s][:, :w], in0=fq[s][:, :w], scalar1=kmax2[:, 0:1], scalar2=1e-30,
                                                               op0=ALU.mult, op1=ALU.add), [bfq[s], bkmax], [bfq[s]])
                cx.op(ACT, lambda e, s=s, w=w: e.activation(out=fq[s][:, :w], in_=fq[s][:, :w], func=AF.Ln), [bfq[s]], [bfq[s]])
                cx.op(ACT, lambda e, s=s, w=w: e.activation(out=fq[s][:, :w], in_=fq[s][:, :w], func=AF.Exp, scale=0.5), [bfq[s]], [bfq[s]])
                cx.op(DVE, lambda e, s=s, w=w: e.tensor_scalar(out=fb[s][:, :w], in0=fq[s][:, :w], scalar1=-1.0, scalar2=None, op0=ALU.mult),
                      [bfq[s]], [bfb[s]])
                cx.dma(SP, QS[:, 96, c0:c0 + w], fb[s][:, :w], [bfb[s]], [bQS], bfb[s])
        chk(f"FX_{l}")

        if True:
            diag = sb(es, "diag", [128, 2, CW, 128], BF16)
            bdiag = Buf("diag")
            gin = [sb(es, f"gin{i}", [128, 2, 544], BF16) for i in range(2)]
            bgin = [Buf(f"gin{i}") for i in range(2)]
            y32 = sb(es, "y32", [128, 2, 512], F32)
            by32 = Buf("y32")
            ysq = [sb(es, f"ysq{i}", [128, 512], F32) for i in range(2)]
            bysq = [Buf(f"ysq{i}") for i in range(2)]
            mean_sb = sb(es, "mean_sb", [128, 512], F32)
            bmean = Buf("mean")
            var_sb = sb(es, "var_sb", [128, 512], F32)
            bvar = Buf("var")
            ct = [sb(es, f"ct{i}", [128, 512], F32) for i in range(2)]
            bct = [Buf(f"ct{i}") for i in range(2)]
            co = [sb(es, f"co{i}", [128, 2, 512], BF16) for i in range(2)]
            bco = [Buf(f"co{i}") for i in range(2)]
            for c in range(2):
                for jt in range(CW):
                    cx.op(DVE, lambda e, c=c, jt=jt: e.tensor_scalar(out=diag[:, c, jt, :], in0=ident[:, :],
                                                                      scalar1=vecs[:, VB + O_CONVW + c * CW + jt:VB + O_CONVW + c * CW + jt + 1],
                                                                      scalar2=None, op0=ALU.mult), [bident, bvecs], [bdiag])
            p2tiles = [t for t in tiles if not (last and t[2])]

            def p2_load(i):
                t0, W, isc = p2tiles[i]
                gcol = (16 if isc else 48) + t0 - 15
                cx.dma(SP, gin[i % 2][:, :, 0:W + 30], GS[:, gcol:gcol + W + 30].rearrange("(c p) t -> p c t", p=128), [bGS], [bgin[i % 2]], bgin[i % 2])

            p2_load(0)
            for i, (t0, W, isc) in enumerate(p2tiles):
                s = i % 2
                if i + 1 < len(p2tiles):
                    p2_load(i + 1)
                for c in range(2):
                    cx.mm(bPS[c], [(bank(c)[:, :W], diag[:, c, jt, :], gin[s][:, c, jt:jt + W]) for jt in range(CW)], [bdiag, bgin[s]])
                    cx.op(ACT, lambda e, c=c: e.activation(out=y32[:, c, :W], in_=bank(c)[:, :W], func=AF.Identity,
                                                            bias=vecs[:, VB + O_CONVB + c:VB + O_CONVB + c + 1], scale=1.0), [bPS[c], bvecs], [by32])
                for c in range(2):
                    cx.mm_acc(bPS[2], bank(2)[:, :W], ones256[:, :], y32[:, c, :W], [by32, bconst], start=(c == 0), stop=(c == 1))
                cx.op(ACT, lambda e: e.activation(out=mean_sb[:, :W], in_=bank(2)[:, :W], func=AF.Copy), [bPS[2]], [bmean])
                for c in range(2):
                    cx.op(DVE, lambda e, c=c: e.tensor_tensor(out=y32[:, c, :W], in0=y32[:, c, :W], in1=mean_sb[:, :W], op=ALU.subtract), [by32, bmean], [by32])
                    cx.op(DVE, lambda e, c=c: e.tensor_tensor(out=ysq[c][:, :W], in0=y32[:, c, :W], in1=y32[:, c, :W], op=ALU.mult), [by32], [bysq[c]])
                    cx.mm_acc(bPS[3], bank(3)[:, :W], ones256[:, :], ysq[c][:, :W], [bysq[c], bconst], start=(c == 0), stop=(c == 1))
                rstd_from(bank(3)[:, :W], bPS[3], var_sb[:, :W], bvar, mean_sb[:, :W], bmean)
                for c in range(2):
                    T = ct[c][:, :W]
                    cx.op(DVE, lambda e, c=c: e.tensor_tensor(out=y32[:, c, :W], in0=y32[:, c, :W], in1=var_sb[:, :W], op=ALU.mult), [by32, bvar], [by32])
                    cx.op(DVE, lambda e, c=c: e.tensor_scalar(out=y32[:, c, :W], in0=y32[:, c, :W],
                                                              scalar1=vecs[:, VB + O_CLNG + c:VB + O_CLNG + c + 1],
                                                              scalar2=vecs[:, VB + O_CLNB + c:VB + O_CLNB + c + 1], op0=ALU.mult, op1=ALU.add), [by32, bvecs], [by32])
                    cx.op(ACT, lambda e, c=c, T=T: e.activation(out=T, in_=y32[:, c, :W], func=AF.Exp, scale=-1.0), [by32], [bct[c]])
                    cx.op(DVE, lambda e, T=T: e.tensor_scalar_add(out=T, in0=T, scalar1=1.0), [bct[c]], [bct[c]])
                    cx.op(DVE, lambda e, T=T: e.reciprocal(out=T, in_=T), [bct[c]], [bct[c]])
                    cx.op(DVE, lambda e, c=c, T=T: e.tensor_tensor(out=co[s][:, c, :W], in0=T, in1=y32[:, c, :W], op=ALU.mult), [bct[c], by32], [bco[s]])
                cx.dma(SP, YS[512:768, t0:t0 + W].rearrange("(c p) t -> p c t", p=128), co[s][:, :, :W], [bco[s]], [bYS], bco[s])
        chk(f"P2_{l}")

        if True:
            Kt = [sb(es, f"Kt{i}", [128, NT], BF16) for i in range(2)]
            bKt = [Buf(f"Kt{i}") for i in range(2)]
            Vt = [sb(es, f"Vt{i}", [128, NBLK, 65], BF16) for i in range(2)]
            bVt = [Buf(f"Vt{i}") for i in range(2)]
            qtt = [sb(es, f"qtt{i}", [128, 512], BF16) for i in range(3)]
            bqtt = [Buf(f"qtt{i}") for i in range(3)]
            GRP = 3
            pt = [sb(es, f"pt{i}", [128, 512 * GRP], BF16) for i in range(3)]
            bpt = [Buf(f"pt{i}") for i in range(3)]
            osb = [sb(es, f"osb{i}", [128, 512], F32) for i in range(2)]
            bosb = [Buf(f"osb{i}") for i in range(2)]
            rden = [sb(es, f"rden{i}", [128, 512], F32) for i in range(2)]
            brden = [Buf(f"rden{i}") for i in range(2)]
            ao = [sb(es, f"ao{i}", [128, 512], BF16) for i in range(2)]
            bao = [Buf(f"ao{i}") for i in range(2)]
            bpair = [Buf(f"pair{i}", psum=True) for i in range(2)]
            qtiles = [t for t in tiles if not (last and t[2])]
            blocks = [(hh, qi) for hh in range(HEADS) for qi in range(len(qtiles))]

            def load_kv(hh):
                s = hh % 2
                cx.dma(SP, Kt[s][0:64, :], KS[hh], [bKS], [bKt[s]], bKt[s])
                cx.dma(SP, Kt[s][64:96, :], KR, [bKR], [bKt[s]], bKt[s])
                cx.dma(SP, Kt[s][96:97, :], KONE, [bKONE], [bKt[s]], bKt[s])
                cx.dma(SP, Vt[s][:, :, :], VS[hh], [bVS], [bVt[s]], bVt[s])

            def load_q(bi):
                hh, qi = blocks[bi]
                t0, W, isc = qtiles[qi]
                cx.dma(SP, qtt[bi % 3][0:96, :W], QS[hh, 0:96, t0:t0 + W], [bQS], [bqtt[bi % 3]], bqtt[bi % 3])
                cx.dma(SP, qtt[bi % 3][96:97, :W], QS[hh, 96:97, t0:t0 + W], [bQS], [bqtt[bi % 3]], bqtt[bi % 3])

            steps = []
            for bi, (hh, qi) in enumerate(blocks):
                t0, W, isc = qtiles[qi]
                nblk = (NCTX // 128) if isc else NBLK
                grps = [list(range(g0, min(g0 + GRP, nblk))) for g0 in range(0, nblk, GRP)]
                for gi, blks in enumerate(grps):
                    steps.append((bi, hh, qi, gi, len(grps), blks))

            def slot_ap(n):
                return PSALL[:, (n % 2) * 512 * GRP:(n % 2 + 1) * 512 * GRP]

            def emit_S(n):
                bi, hh, qi, gi, ng, blks = steps[n]
                t0, W, isc = qtiles[qi]
                ps = slot_ap(n)
                E = PE
                wr_ = [bpair[n % 2]] + ([bPS[GRP * (n % 2) + i_] for i_ in range(GRP)] if n < 2 else [])
                E.wait(cx._deps([bKt[hh % 2], bqtt[bi % 3]], wr_))
                ins = None
                for jj, blk in enumerate(blks):
                    ins = E.e.matmul(ps[:, jj * W:(jj + 1) * W], Kt[hh % 2][0:97, blk * 128:(blk + 1) * 128], qtt[bi % 3][0:97, :W], start=True, stop=True)
                E.cnt += 1
                ins.then_inc(E.sem, 1)
                cx._record((E.sem, E.cnt), [bKt[hh % 2], bqtt[bi % 3]], [bpair[n % 2]])
                nw = len(blks) * W
                cx.op(ACT, lambda e: e.activation(out=pt[n % 3][:, 0:nw], in_=ps[:, 0:nw], func=AF.Exp, scale=ATTN_SCALE),
                      [bpair[n % 2]], [bpt[n % 3]])

            pend = []

            def emit_PV(n):
                bi, hh, qi, gi, ng, blks = steps[n]
                t0, W, isc = qtiles[qi]
                for jj, blk in enumerate(blks):
                    cx.mm_acc(bPS[6], bank(6)[0:65, :W], Vt[hh % 2][:, blk, :],
                              pt[n % 3][:, jj * W:(jj + 1) * W], [bVt[hh % 2], bpt[n % 3]],
                              start=(gi == 0 and jj == 0), stop=(gi == ng - 1 and jj == len(blks) - 1), mark=(jj == len(blks) - 1))
                if gi == ng - 1:
                    es_ = bi % 2
                    cx.op(DVE, lambda e: e.tensor_copy(out=osb[es_][0:65, :W], in_=bank(6)[0:65, :W]), [bPS[6]], [bosb[es_]])
                    cx.op(DVE, lambda e: e.reciprocal(out=rden[es_][64:65, :W], in_=osb[es_][64:65, :W]), [bosb[es_]], [brden[es_]])

                    def e2(es_=es_, W=W, hh=hh, t0=t0):
                        cx.mm(bPS[7], [(bank(7)[0:64, :W], ones1[64:65, 0:64], rden[es_][64:65, :W])], [bconst, brden[es_]])
                        cx.op(DVE, lambda e: e.tensor_tensor(out=ao[es_][0:64, :W], in0=osb[es_][0:64, :W], in1=bank(7)[0:64, :W], op=ALU.mult),
                              [bosb[es_], bPS[7]], [bao[es_]])
                        cx.dma(SP, YS[hh * 64:(hh + 1) * 64, t0:t0 + W], ao[es_][0:64, :W], [bao[es_]], [bYS], bao[es_])
                    pend.append([3, e2])

            load_kv(0)
            load_q(0)
            if len(blocks) > 1:
                load_q(1)
            LOOK = 1
            for n in range(len(steps) + LOOK):
                if n < len(steps):
                    bi, hh, qi, gi, ng, blks = steps[n]
                    if gi == 0:
                        if bi + 2 < len(blocks):
                            load_q(bi + 2)
                    emit_S(n)
                for p_ in pend:
                    p_[0] -= 1
                while pend and pend[0][0] <= 0:
                    pend.pop(0)[1]()
                if n - LOOK >= 0:
                    emit_PV(n - LOOK)
                    bi_, hh_, qi_, gi_, ng_, blks_ = steps[n - LOOK]
                    if qi_ == 0 and gi_ == 0 and hh_ + 1 < HEADS:
                        load_kv(hh_ + 1)
            while pend:
                pend.pop(0)[1]()
            cx.barrier()
        es_mid.close()
        chk(f"P3_{l}")

        with ExitStack() as es:
            w_out_sb = sb(es, "w_out_sb", [128, 8, D], BF16)
            bwo = Buf("w_out")
            w1b = [sb(es, f"w1b{i}", [128, 8, 512], BF16) for i in range(2)]
            bw1b = [Buf(f"w1b{i}") for i in range(2)]
            w2b = [sb(es, f"w2b{i}", [128, 32, 128], BF16) for i in range(2)]
            bw2b = [Buf(f"w2b{i}") for i in range(2)]
            xt = [sb(es, f"x4_{i}", [128, 8, 512], F32) for i in range(2)]
            bxt = [Buf(f"x4_{i}") for i in range(2)]
            yc = [sb(es, f"yc{i}", [128, 8, 512], BF16) for i in range(2)]
            byc = [Buf(f"yc{i}") for i in range(2)]
            h2 = sb(es, "h2", [128, 8, 512], BF16)
            bh2 = Buf("h2")
            u_t = sb(es, "u_t", [128, 32, 512], BF16)
            bu = [Buf(f"u{i}") for i in range(32)]
            rl = [sb(es, f"rl{i}", [128, 512], F32) for i in range(2)]
            brl = [Buf(f"rl{i}") for i in range(2)]
            sqt = [sb(es, f"sq4_{i}", [128, 512], BF16) for i in range(2)]
            bsq = [Buf(f"sq4_{i}") for i in range(2)]
            t32 = [sb(es, f"t4_{i}", [128, 512], F32) for i in range(2)]
            bt32 = [Buf(f"t4_{i}") for i in range(2)]
            rstd = sb(es, "rstd4", [128, 512], F32)
            brstd = Buf("rstd4")
            cx.dma(SP, w_out_sb[:, :, :], WoutS[l].rearrange("(k p) n -> p k n", p=128), [bWb[l]], [bwo], bwo)
            p4tiles = [t for t in tiles if not (last and t[2])]
            wcnt = [0, 0]

            def p4_load(i):
                t0, W, isc = p4tiles[i]
                s = i % 2
                cx.dma(SP, xt[s][:, :, :W], Xsrc[:, t0:t0 + W].rearrange("(k p) t -> p k t", p=128), [bXsrc], [bxt[s]], bxt[s])
                cx.dma(SP, yc[s][:, :, :W], YS[:, t0:t0 + W].rearrange("(k p) t -> p k t", p=128), [bYS], [byc[s]], byc[s])

            def load_w1(g, slot):
                cx.dma(SP, w1b[slot][:, :, :], W1S[l, g], [bWb[l]], [bw1b[slot]], bw1b[slot])

            def load_w2(c, slot):
                cx.dma(SP, w2b[slot][:, :, :], W2S[l, c], [bWb[l]], [bw2b[slot]], bw2b[slot])

            pend4 = None
            p4_load(0)
            load_w1(0, 0)
            for i, (t0, W, isc) in enumerate(p4tiles):
                s = i % 2
                j = 1 if isc else 0
                for c in range(8):
                    bk = c % 4
                    cx.mm(bPS[bk], [(bank(bk)[:, :W], w_out_sb[:, k, c * 128:(c + 1) * 128], yc[s][:, k, :W]) for k in range(8)], [bwo, byc[s]])
                    cx.op(DVE, lambda e, c=c, bk=bk: e.scalar_tensor_tensor(out=xt[s][:, c, :W], in0=bank(bk)[:, :W], scalar=mod[:, 16 + c, j:j + 1],
                                                                             in1=xt[s][:, c, :W], op0=ALU.mult, op1=ALU.add),
                          [bPS[bk], bmod, bxt[s]], [bxt[s]])
                adanorm(es, xt[s], bxt[s], W, 1, j, h2, bh2, sqt, bsq, rstd, brstd, t32, bt32, 4, sq_act=False, ones_mat=onesDb)
                if pend4 is not None:
                    pend4()
                    pend4 = None
                if i + 1 < len(p4tiles):
                    p4_load(i + 1)
                for g in range(8):
                    slot = wcnt[0] % 2
                    wcnt[0] += 1
                    if g + 1 < 8:
                        load_w1(g + 1, wcnt[0] % 2)
                    else:
                        load_w2(0, wcnt[1] % 2)
                    for jj in range(4):
                        f = g * 4 + jj
                        bk = f % 4
                        cx.mm(bPS[bk], [(bank(bk)[:, :W], w1b[slot][:, k, jj * 128:(jj + 1) * 128], h2[:, k, :W]) for k in range(8)], [bw1b[slot], bh2])
                        cx.op(ACT, lambda e, bk=bk, f=f: e.activation(out=rl[f % 2][:, :W], in_=bank(bk)[:, :W], func=AF.Relu), [bPS[bk]], [brl[f % 2]])
                        cx.op(DVE, lambda e, f=f: e.tensor_tensor(out=u_t[:, f, :W], in0=rl[f % 2][:, :W], in1=rl[f % 2][:, :W], op=ALU.mult), [brl[f % 2]], [bu[f]])

                def p4_tail(i=i, s=s, j=j, t0=t0, W=W):
                    for c in range(8):
                        slot = wcnt[1] % 2
                        wcnt[1] += 1
                        if c + 1 < 8:
                            load_w2(c + 1, wcnt[1] % 2)
                        elif i + 1 < len(p4tiles):
                            load_w1(0, wcnt[0] % 2)
                        bk = 4 + c % 4
                        cx.mm(bPS[bk], [(bank(bk)[:, :W], w2b[slot][:, k, :], u_t[:, k, :W]) for k in range(32)], [bw2b[slot]] + bu)
                        cx.op(DVE, lambda e, c=c, bk=bk: e.scalar_tensor_tensor(out=xt[s][:, c, :W], in0=bank(bk)[:, :W], scalar=mod[:, 40 + c, j:j + 1],
                                                                                 in1=xt[s][:, c, :W], op0=ALU.mult, op1=ALU.add),
                              [bPS[bk], bmod, bxt[s]], [bxt[s]])
                    if not last:
                        cx.dma(SP, XS[:, t0:t0 + W].rearrange("(k p) t -> p k t", p=128), xt[s][:, :, :W], [bxt[s]], [bXS], bxt[s])
                    else:
                        FO = L * VL
                        for c in range(8):
                            cx.op(DVE, lambda e, c=c: e.tensor_tensor(out=sqt[c % 2][:, :W], in0=xt[s][:, c, :W], in1=xt[s][:, c, :W], op=ALU.mult), [bxt[s]], [bsq[c % 2]])
                            cx.mm_acc(bPS[0], bank(0)[:, :W], onesDb[:, :], sqt[c % 2][:, :W], [bsq[c % 2], bconst], start=(c == 0), stop=(c == 7))
                        rstd_from(bank(0)[:, :W], bPS[0], rstd[:, :W], brstd, t32[0][:, :W], bt32[0])
                        for c in range(8):
                            cx.op(DVE, lambda e, c=c: e.scalar_tensor_tensor(out=xt[s][:, c, :W], in0=xt[s][:, c, :W], scalar=vecs[:, FO + c:FO + c + 1],
                                                                              in1=rstd[:, :W], op0=ALU.mult, op1=ALU.mult), [bxt[s], bvecs, brstd], [bxt[s]])
                        cx.dma(SP, outT[:, t0 - NCTX:t0 - NCTX + W].rearrange("(k p) t -> p k t", p=128), xt[s][:, :, :W], [bxt[s]], [Buf("o")], bxt[s])
                pend4 = p4_tail
            if pend4 is not None:
                pend4()
            cx.barrier()

    cx.barrier()
    es_global.close()
    return nc


def _cols(v):
    v = np.asarray(v, np.float32)
    return np.ascontiguousarray(v.reshape(-1, 128).T)


def _rope_tables(S):
    NT = NCTX + S
    rows = S // GRID_W
    row = np.repeat(np.arange(rows, dtype=np.int32), GRID_W).astype(np.float32)
    col = np.tile(np.arange(GRID_W, dtype=np.int32), rows).astype(np.float32)
    n = 8
    inv = (1.0 / (np.float32(10000.0) ** (np.arange(n, dtype=np.float32) / n))).astype(np.float32)
    C = np.ones((32, NT), np.float32)
    Sg = np.zeros((32, NT), np.float32)
    for d in range(32):
        a, jj = d // 16, d % 16
        b_, i = jj // 8, jj % 8
        pos = row if a == 0 else col
        ang = (pos * inv[i]).astype(np.float32)
        C[d, NCTX:] = np.cos(ang)
        Sg[d, NCTX:] = (-1.0 if b_ == 0 else 1.0) * np.sin(ang)
    return C, Sg


_PROG_CACHE = {}


def kernel_impl(inputs, S, DEPTH, ncores):
    f = lambda k: np.asarray(inputs[k], np.float32)
    L = DEPTH
    x, c, ctx, c_ctx = f("x"), f("c"), f("ctx"), f("c_ctx")
    NT = NCTX + S
    ropeC, ropeS = _rope_tables(S)
    ident = np.eye(128, dtype=np.float32)
    ada_w = np.ascontiguousarray(f("ada_w")[:L])
    w_in = np.ascontiguousarray(f("w_in")[:L])
    w_uq = np.ascontiguousarray(f("w_uq")[:L])
    w_ukv = np.ascontiguousarray(f("w_ukv")[:L])
    w_out = np.ascontiguousarray(f("w_out")[:L])
    w_ff1 = np.ascontiguousarray(f("w_ff1")[:L])
    w_ff2 = np.ascontiguousarray(f("w_ff2")[:L])
    sgu_w = f("sgu_w")[:L]
    wsT = np.ascontiguousarray(sgu_w.transpose(0, 3, 1, 2))
    sgb = np.stack([np.broadcast_to(np.stack([f("sgu_ln_g")[l], f("sgu_ln_b")[l]], 0)[None], (128, 2, 256)) for l in range(L)], 0)
    sgb = np.ascontiguousarray(sgb, dtype=np.float32)
    sgu_b = f("sgu_b")[:L]
    bsT = np.zeros((L, 128, 2, 128), np.float32)
    for cc in range(2):
        for hh in range(2):
            bsT[:, hh * 64:(hh + 1) * 64, cc, :] = sgu_b[:, 2 * cc + hh, None, :]
    per_layer = []
    for l in range(L):
        cw = f("conv_w")[l]
        convw = np.concatenate([cw[:, cc * 128:(cc + 1) * 128].T for cc in range(2)], 1)
        per_layer.append(np.concatenate([
            _cols(f("norm1_g")[l]), _cols(f("norm2_g")[l]), _cols(f("q_norm_g")[l]), _cols(f("kv_norm_g")[l]),
            _cols(f("conv_b")[l]), _cols(f("conv_ln_g")[l]), _cols(f("conv_ln_b")[l]), _cols(f("ada_b")[l]), convw], 1))
    shared = {"ropeC": ropeC, "ropeS": ropeS, "ident": ident, "ada_w": ada_w, "w_in": w_in, "w_uq": w_uq, "w_ukv": w_ukv,
              "wsT": wsT, "sgb": sgb, "bsT": bsT, "w_out": w_out, "w_ff1": w_ff1, "w_ff2": w_ff2}
    in_maps = []
    for b in range(ncores):
        sv = np.stack([_cols(c[b]), _cols(c_ctx)], -1).reshape(128, 16)
        vecs = np.concatenate(per_layer + [_cols(f("final_g")), sv], 1).astype(np.float32)
        xT = np.ascontiguousarray(np.concatenate([ctx[b].T, x[b].T], 1))
        m = dict(shared)
        m["vecs"] = np.ascontiguousarray(vecs)
        m["xT"] = xT
        in_maps.append(m)
    key = (S, DEPTH)
    if key not in _PROG_CACHE:
        _PROG_CACHE[key] = build_program(S, DEPTH)
    nc = _PROG_CACHE[key]
    res = run_bass_kernel_spmd(nc, in_maps, core_ids=list(range(ncores)))
    out = np.stack([np.ascontiguousarray(r["outT"].T) for r in res.results], 0)
    return out.astype(np.float32)


def kernel(**inputs):
    return kernel_impl(inputs, 8192, 4, 8)
```
